# Optimizing a Trainium2 kernel written in Bass

```python
import math
import jax, jax.numpy as jnp
from jax import lax
import numpy as np

D_MODEL = 1024
BATCH = 2
SEQ = 8192
DEPTH = 1

MIX_WIDTH = D_MODEL
ATTN_HEADS = 4
QK_HEAD_DIM = 64
V_HEAD_DIM = 2 * QK_HEAD_DIM
ATTN_WIDTH = ATTN_HEADS * V_HEAD_DIM
QK_WIDTH = ATTN_HEADS * 2 * QK_HEAD_DIM
ROT_DIM = QK_HEAD_DIM // 4
ROPE_THETA = 500000.0
QBLOCK = 128
RNN_WIDTH = MIX_WIDTH - ATTN_WIDTH
RNN_HEADS = 8
RNN_BLOCK = RNN_WIDTH // RNN_HEADS
RNN_CONV_WIDTH = 4
LRU_C = 8.0
IN_PROJ_WIDTH = 2 * QK_WIDTH + ATTN_WIDTH + 2 * RNN_WIDTH
D_FF = 3 * D_MODEL
FFN_CONV_WIDTH = 3
EPS = 1e-6

kernel_name = "hybrid_diffattn_rglru_convffn"


def rms_norm(x, w):
    xf = x.astype(jnp.float32)
    y = xf * lax.rsqrt(jnp.mean(xf * xf, axis=-1, keepdims=True) + EPS)
    return (y * w.astype(jnp.float32)).astype(x.dtype)


def causal_dwconv(x, w, b):
    width = w.shape[0]
    c = x.shape[-1]
    y = lax.conv_general_dilated(
        x, w[:, None, :].astype(x.dtype), window_strides=(1,),
        padding=[(width - 1, 0)], dimension_numbers=('NWC', 'WIO', 'NWC'),
        feature_group_count=c)
    return y + b.astype(x.dtype)


def rope_tables(seq):
    pos = jnp.arange(seq, dtype=jnp.float32)
    inv_freq = ROPE_THETA ** (-(jnp.arange(0, ROT_DIM, 2, dtype=jnp.float32) / ROT_DIM))
    ang = pos[:, None] * inv_freq[None, :]
    return jnp.cos(ang), jnp.sin(ang)


def partial_rope(t, cos, sin):
    c = cos[None, :, None, None, :].astype(t.dtype)
    s = sin[None, :, None, None, :].astype(t.dtype)
    half = ROT_DIM // 2
    x1 = t[..., :half]
    x2 = t[..., half:ROT_DIM]
    rot = jnp.concatenate([x1 * c - x2 * s, x2 * c + x1 * s], axis=-1)
    return jnp.concatenate([rot, t[..., ROT_DIM:]], axis=-1)


def diff_attention(q, k, v, lam):
    b, s, h, _, dh = q.shape
    dv = v.shape[-1]
    qt = q.transpose(0, 2, 3, 1, 4)
    kt = k.transpose(0, 2, 3, 1, 4)
    vt = v.transpose(0, 2, 1, 3)
    n_blocks = s // QBLOCK
    kpos = jnp.arange(s)
    scale = dh ** -0.5

    def block(i):
        start = i * QBLOCK
        qb = lax.dynamic_slice_in_dim(qt, start, QBLOCK, axis=3)
        sc = jnp.einsum('bhcqd,bhckd->bhcqk', qb, kt).astype(jnp.float32) * scale
        qpos = start + jnp.arange(QBLOCK)
        mask = kpos[None, :] <= qpos[:, None]
        sc = jnp.where(mask, sc, -jnp.inf)
        p = jax.nn.softmax(sc, axis=-1)
        a = p[:, :, 0] - lam * p[:, :, 1]
        return jnp.einsum('bhqk,bhkv->bhqv', a.astype(vt.dtype), vt)

    out = lax.map(block, jnp.arange(n_blocks))
    return out.transpose(1, 0, 3, 2, 4).reshape(b, s, h, dv)


def rg_lru(x, wa, ba, wx, bx, lru_lambda):
    b, s, c = x.shape
    xb = x.reshape(b, s, RNN_HEADS, RNN_BLOCK)
    r = jax.nn.sigmoid(jnp.einsum('bshi,hij->bshj', xb, wa) + ba).reshape(b, s, c).astype(jnp.float32)
    ig = jax.nn.sigmoid(jnp.einsum('bshi,hij->bshj', xb, wx) + bx).reshape(b, s, c).astype(jnp.float32)
    log_a = -LRU_C * r * jax.nn.softplus(-lru_lambda.astype(jnp.float32))
    a = jnp.exp(log_a)
    mult = jnp.sqrt(-jnp.expm1(2.0 * log_a))
    u = mult * ig * x.astype(jnp.float32)

    def combine(left, right):
        a1, b1 = left
        a2, b2 = right
        return a1 * a2, a2 * b1 + b2

    _, hs = lax.associative_scan(combine, (a, u), axis=1)
    return hs.astype(x.dtype)


def setup_inputs(seed: int = 0) -> dict:
    key = jax.random.key(seed)
    ks = jax.random.split(key, 32)
    L = DEPTH

    def nrm(k, shape, scale):
        return jax.random.normal(k, shape, jnp.float32) * scale

    u = jax.random.uniform(ks[14], (L, RNN_WIDTH), jnp.float32, minval=0.9, maxval=0.999)
    sig = u ** (1.0 / LRU_C)
    lru_lambda = jnp.log(sig) - jnp.log1p(-sig)
    return {
        "x": nrm(ks[0], (BATCH, SEQ, D_MODEL), 1.0),
        "norm1_w": 1.0 + nrm(ks[1], (L, D_MODEL), 0.01),
        "w_in": nrm(ks[2], (L, D_MODEL, IN_PROJ_WIDTH), D_MODEL ** -0.5),
        "q_norm_w": 1.0 + nrm(ks[3], (L, QK_HEAD_DIM), 0.01),
        "k_norm_w": 1.0 + nrm(ks[4], (L, QK_HEAD_DIM), 0.01),
        "lambda_q1": nrm(ks[5], (L, QK_HEAD_DIM), 0.1),
        "lambda_k1": nrm(ks[6], (L, QK_HEAD_DIM), 0.1),
        "lambda_q2": nrm(ks[7], (L, QK_HEAD_DIM), 0.1),
        "lambda_k2": nrm(ks[8], (L, QK_HEAD_DIM), 0.1),
        "subln_w": 1.0 + nrm(ks[9], (L, V_HEAD_DIM), 0.01),
        "conv_rnn_w": nrm(ks[10], (L, RNN_CONV_WIDTH, RNN_WIDTH), RNN_CONV_WIDTH ** -0.5),
        "conv_rnn_b": nrm(ks[11], (L, RNN_WIDTH), 0.01),
        "w_gate_a": nrm(ks[12], (L, RNN_HEADS, RNN_BLOCK, RNN_BLOCK), RNN_BLOCK ** -0.5),
        "b_gate_a": nrm(ks[13], (L, RNN_HEADS, RNN_BLOCK), 0.01),
        "w_gate_x": nrm(ks[15], (L, RNN_HEADS, RNN_BLOCK, RNN_BLOCK), RNN_BLOCK ** -0.5),
        "b_gate_x": nrm(ks[16], (L, RNN_HEADS, RNN_BLOCK), 0.01),
        "lru_lambda": lru_lambda,
        "rnn_norm_w": 1.0 + nrm(ks[17], (L, RNN_WIDTH), 0.01),
        "w_out": nrm(ks[18], (L, MIX_WIDTH, D_MODEL), MIX_WIDTH ** -0.5),
        "norm2_w": 1.0 + nrm(ks[19], (L, D_MODEL), 0.01),
        "w_up": nrm(ks[20], (L, D_MODEL, 2 * D_FF), D_MODEL ** -0.5),
        "conv_ffn_w": nrm(ks[21], (L, FFN_CONV_WIDTH, 2 * D_FF), FFN_CONV_WIDTH ** -0.5),
        "conv_ffn_b": nrm(ks[22], (L, 2 * D_FF), 0.01),
        "w_down": nrm(ks[23], (L, D_FF, D_MODEL), D_FF ** -0.5),
    }


def reference(x, norm1_w, w_in, q_norm_w, k_norm_w, lambda_q1, lambda_k1, lambda_q2,
              lambda_k2, subln_w, conv_rnn_w, conv_rnn_b, w_gate_a, b_gate_a, w_gate_x,
              b_gate_x, lru_lambda, rnn_norm_w, w_out, norm2_w, w_up, conv_ffn_w,
              conv_ffn_b, w_down):
    b, s, _ = x.shape
    cos, sin = rope_tables(s)
    for l in range(DEPTH):
        lambda_init = 0.8 - 0.6 * math.exp(-0.3 * l)
        h = rms_norm(x, norm1_w[l])
        proj = h @ w_in[l]
        q, k, v, xr, gr = jnp.split(
            proj, [QK_WIDTH, 2 * QK_WIDTH, 2 * QK_WIDTH + ATTN_WIDTH,
                   2 * QK_WIDTH + ATTN_WIDTH + RNN_WIDTH], axis=-1)
        q = q.reshape(b, s, ATTN_HEADS, 2, QK_HEAD_DIM)
        k = k.reshape(b, s, ATTN_HEADS, 2, QK_HEAD_DIM)
        v = v.reshape(b, s, ATTN_HEADS, V_HEAD_DIM)
        q = partial_rope(rms_norm(q, q_norm_w[l]), cos, sin)
        k = partial_rope(rms_norm(k, k_norm_w[l]), cos, sin)
        lam = (jnp.exp(jnp.sum(lambda_q1[l].astype(jnp.float32) * lambda_k1[l].astype(jnp.float32)))
               - jnp.exp(jnp.sum(lambda_q2[l].astype(jnp.float32) * lambda_k2[l].astype(jnp.float32)))
               + lambda_init)
        attn = diff_attention(q, k, v, lam)
        attn = rms_norm(attn, subln_w[l]) * (1.0 - lambda_init)
        attn = attn.reshape(b, s, ATTN_WIDTH)
        xc = causal_dwconv(xr, conv_rnn_w[l], conv_rnn_b[l])
        y = rg_lru(xc, w_gate_a[l], b_gate_a[l], w_gate_x[l], b_gate_x[l], lru_lambda[l])
        y = rms_norm(y * jax.nn.gelu(gr), rnn_norm_w[l])
        x = x + jnp.concatenate([attn, y], axis=-1) @ w_out[l]
        h2 = rms_norm(x, norm2_w[l])
        up = causal_dwconv(h2 @ w_up[l], conv_ffn_w[l], conv_ffn_b[l])
        g, val = jnp.split(up, 2, axis=-1)
        x = x + (jax.nn.gelu(g) * val) @ w_down[l]
    return x
```

```python
import math
from contextlib import ExitStack
import numpy as np
import concourse.bass as bass
import concourse.mybir as mybir
from concourse.bass_utils import run_bass_kernel_spmd

F32 = mybir.dt.float32
BF16 = mybir.dt.bfloat16
I32 = mybir.dt.int32
AF = mybir.ActivationFunctionType
ALU = mybir.AluOpType

D = 1024
S = 8192
NCH = 32
CW = 256
W = 258
NSLOT = 8
EPS = 1e-6
LAMBDA_INIT = 0.8 - 0.6 * math.exp(0.0)
SAME_SYNC = True
RAW_ONLY_SAME = True
EMBED_WAIT = True
N_A_TILES = 32

PC = {}
_o = 0
for _n, _w in [("n1w", 8), ("qw", 1), ("kw", 1), ("lam", 4), ("subw", 1), ("crw", 16), ("crb", 4),
               ("bga", 4), ("bgx", 4), ("lru", 4), ("rnw", 4), ("n2w", 8), ("cfw", 144), ("cfb", 48),
               ("sel", 4), ("hflag", 1), ("qpos", 8), ("kpos", 32), ("fturn", 1)]:
    PC[_n] = (_o, _w)
    _o += _w
NP = _o


class T:
    __slots__ = ("ap", "name", "last_w", "readers", "sem", "cnt")

    def __init__(self, ap, name):
        self.ap = ap
        self.name = name
        self.last_w = None
        self.readers = {}
        self.sem = None
        self.cnt = 0


class Ins:
    __slots__ = ("eng", "fn", "deps", "is_dma", "sem", "val", "signal", "sigval")

    def __init__(self, eng, fn):
        self.eng = eng
        self.fn = fn
        self.deps = []
        self.is_dma = False
        self.sem = None
        self.val = 0
        self.signal = False
        self.sigval = 0


class Prog:
    ENGS = ("pe", "act", "dve", "pool", "sp")

    def __init__(self, nc, es):
        self.nc = nc
        self.es = es
        self.ins = {e: [] for e in self.ENGS}
        self.nsem = 0

    def new_sem(self, name):
        self.nsem += 1
        return self.es.enter_context(self.nc.semaphore(name))

    def op(self, eng, fn, reads=(), writes=(), dma=False, nowaw=False):
        ins = Ins(eng, fn)
        ins.is_dma = dma
        deps = []
        raw = set()
        for t in reads:
            if t.last_w is not None:
                deps.append(t.last_w)
                raw.add(id(t.last_w))
        for t in writes:
            if t.last_w is not None and not (nowaw and t.last_w.is_dma and t.last_w.eng == eng):
                deps.append(t.last_w)
            deps.extend(t.readers.values())
        seen = set()
        for d in deps:
            if id(d) in seen or d is ins:
                continue
            seen.add(id(d))
            if not d.is_dma and d.eng == eng and (eng == "pe" or not SAME_SYNC):
                continue
            if RAW_ONLY_SAME and not d.is_dma and d.eng == eng and id(d) not in raw:
                continue
            ins.deps.append(d)
            d.signal = True
        if dma:
            t = writes[0]
            if t.sem is None:
                t.sem = self.new_sem("d_" + t.name)
            t.cnt += 1
            ins.sem = t.sem
            ins.val = 16 * t.cnt
        for t in reads:
            t.readers[eng if not dma else ("dma", id(ins))] = ins
        for t in writes:
            t.last_w = ins
            t.readers = {}
        self.ins[eng].append(ins)
        return ins

    def emit(self, final_waits):
        nc = self.nc
        esem = {e: self.es.enter_context(nc.semaphore("e_" + e)) for e in self.ENGS}
        for e in self.ENGS:
            c = 0
            for i in self.ins[e]:
                if i.signal and not i.is_dma:
                    c += 1
                    i.sigval = c
        block = self.es.enter_context(nc.Block())

        def run(e, eng):
            seen = {}
            for i in self.ins[e]:
                need = {}
                for d in i.deps:
                    if d.is_dma:
                        key, v, sem = ("dma", id(d.sem)), d.val, d.sem
                    else:
                        key, v, sem = d.eng, d.sigval, esem[d.eng]
                    if seen.get(key, 0) >= v:
                        continue
                    seen[key] = v
                    need[key] = (sem, v)
                waits = list(need.values())
                emb = None
                if waits and EMBED_WAIT:
                    emb = waits.pop()
                for sem, v in waits:
                    eng.wait_ge(sem, v)
                bi = i.fn(eng)
                if emb is not None:
                    bi = bi._wait_ge(emb[0], emb[1])
                if i.is_dma:
                    bi.then_inc(i.sem, 16)
                elif i.signal:
                    bi.then_inc(esem[e], 1)
            if e == "sp":
                for t in final_waits:
                    eng.wait_ge(t.sem, 16 * t.cnt)

        @block.tensor
        def _(eng):
            run("pe", eng)

        @block.scalar
        def _(eng):
            run("act", eng)

        @block.vector
        def _(eng):
            run("dve", eng)

        @block.gpsimd
        def _(eng):
            run("pool", eng)

        @block.sync
        def _(eng):
            run("sp", eng)


def build():
    nc = bass.Bass("TRN2", target_bir_lowering=False, dynamic_dma_scratch_size=512)
    es = ExitStack()
    P = Prog(nc, es)

    def din(name, shape, dt=F32):
        return nc.dram_tensor(name, shape, dt, kind="ExternalInput").ap()

    xT = din("xT", [D, S])
    xoT = din("xoT", [D, NSLOT * W])
    w_in = din("w_in", [D, 2560])
    w_out = din("w_out", [D, D])
    w_up = din("w_up", [D, 6144])
    w_down = din("w_down", [3072, D])
    params_d = din("params", [128, NP])
    wg_d = din("wg", [128, 8 * 128])
    cst_d = din("cst", [128, 2 * 128])
    mask_d = din("mask", [128, 9 * W])
    outT = nc.dram_tensor("outT", [D, NSLOT * CW], F32, kind="ExternalOutput").ap()
    win_s = nc.dram_tensor("win_s", [20, 128, 8 * 128], BF16).ap()
    wout_s = nc.dram_tensor("wout_s", [8, 128, 8 * 128], BF16).ap()
    wup_s = nc.dram_tensor("wup_s", [48, 128, 8 * 128], BF16).ap()
    wdn_s = nc.dram_tensor("wdn_s", [8, 128, 24 * 128], BF16).ap()
    xmid_s = nc.dram_tensor("xmid_s", [NSLOT, 128, 8 * W], F32).ap()

    def sb(name, shape, dt):
        return nc.alloc_sbuf_tensor(name, shape, dt)

    def tile(name, shape, dt):
        h = sb(name, shape, dt)
        return T(h, name)

    arena = sb("arena", [128, 65536], BF16)
    KT = arena[:, 0:32768].rearrange("p (h s) -> p h s", h=4)
    KTt = [[T(KT, f"KT{h}_{c}") for c in range(NCH)] for h in range(4)]
    VV = arena[:, 32768:65536].rearrange("p (b x) -> p b x", x=512)
    VVt = [T(VV, f"V{b}") for b in range(64)]
    prm = tile("prm", [128, NP], F32)
    wgb = tile("wgb", [128, 8, 128], BF16)
    cstb = tile("cstb", [128, 2, 128], BF16)
    maskb = tile("maskb", [128, 9, W], BF16)
    ones_bf = tile("ones_bf", [128, 128], BF16)
    bones_bf = tile("bones_bf", [128, 128], BF16)
    ones_f = tile("ones_f", [128, 128], F32)
    dC = tile("dC", [128, W], F32)
    dS = tile("dS", [128, W], F32)
    c0s0 = tile("c0s0", [128, 80], F32)
    dvc = tile("dvc", [128, 16], F32)
    hbuf = [tile(f"hbuf{c}", [128, W], F32) for c in range(4)]
    xr = [tile(f"xr{c}", [128, 260], BF16) for c in range(4)]
    yown = [tile(f"yown{c}", [128, W], F32) for c in range(4)]

    xs = [tile(f"xs{k}", [128, W], F32) for k in range(8)]
    xsq = [tile(f"xsq{i}", [128, W], BF16) for i in range(2)]
    xbp = [[tile(f"xb{p}_{k}", [128, W], BF16) for k in range(8)] for p in range(2)]
    Ct = [tile(f"Ct{p}", [128, W], F32) for p in range(2)]
    St = [tile(f"St{p}", [128, W], F32) for p in range(2)]
    rTa = tile("rTa", [128, W], F32)
    rTb = tile("rTb", [128, W], F32)
    rstd_t = tile("rstd", [128, W], F32)
    kc = [dict(rk=tile(f"k_rk{h}", [128, W], F32), kn=tile(f"k_kn{h}", [128, W], F32),
               ksq=tile(f"k_ksq{h}", [128, W], BF16), knb=tile(f"k_knb{h}", [128, W], BF16)) for h in range(4)]
    rc = [dict(xc=tile(f"r_xc{r}", [128, W], F32), r=tile(f"r_r{r}", [128, W], F32),
               ig=tile(f"r_ig{r}", [128, W], F32), a=tile(f"r_a{r}", [128, W], F32),
               xcb=tile(f"r_xcb{r}", [128, W], BF16)) for r in range(2)]
    NWST = 7
    wst = [tile(f"wst{i}", [128, 8, 128], BF16) for i in range(NWST)]
    dgb = tile("dgb", [128, 16, 128], BF16)
    ti32 = T(rTa.ap[:, :].bitcast(I32), "ti32")
    ft = {"kf": kc[0]["rk"], "rk": kc[1]["rk"], "kn": kc[0]["kn"], "kt1": kc[1]["kn"], "kt2": kc[2]["kn"],
          "rS": rTb}

    def tf(name):
        m = {"sc_r": "kf", "sc_nf": "rk", "sc_y": "kn", "su0": "kt1", "su1": "kt2", "iota": "rS"}
        return ft[m.get(name, name)]

    ps = [T(es.enter_context(nc.psum_tensor(f"ps{i}", [128, 512], F32)), f"ps{i}") for i in range(8)]

    def pcol(name, i=0, n=1):
        o, w = PC[name]
        return prm.ap[:, o + i:o + i + n]

    wst_i = [0]

    def next_wst():
        t = wst[wst_i[0] % NWST]
        wst_i[0] += 1
        return t

    def dma(out_t, out_ap, in_ap, reads=(), nowaw=False, q="sp"):
        P.op(q, lambda e: e.dma_start(out=out_ap, in_=in_ap), reads=reads, writes=[out_t], dma=True, nowaw=nowaw)

    def act(out_t, out_ap, in_t, in_ap, func, bias=None, scale=None, extra_reads=()):
        kw = {}
        if bias is not None:
            kw["bias"] = bias
        if scale is not None:
            kw["scale"] = scale
        P.op("act", lambda e: e.activation(out=out_ap, in_=in_ap, func=func, **kw),
             reads=[in_t, prm, dvc] + list(extra_reads), writes=[out_t])

    def mm(out_t, out_ap, lt, l_ap, rt, r_ap, start, stop):
        P.op("pe", lambda e: e.matmul(out_ap, l_ap, r_ap, start=start, stop=stop),
             reads=[lt, rt], writes=[out_t])

    def tt(eng, out_t, out_ap, a_t, a_ap, b_t, b_ap, op):
        P.op(eng, lambda e: e.tensor_tensor(out=out_ap, in0=a_ap, in1=b_ap, op=op),
             reads=[a_t, b_t], writes=[out_t])

    def ts(eng, out_t, out_ap, a_t, a_ap, s1, s2, op0, op1=None, extra_reads=()):
        if op1 is None:
            P.op(eng, lambda e: e.tensor_scalar(out=out_ap, in0=a_ap, scalar1=s1, scalar2=None, op0=op0),
                 reads=[a_t, prm, dvc] + list(extra_reads), writes=[out_t])
        else:
            P.op(eng, lambda e: e.tensor_scalar(out=out_ap, in0=a_ap, scalar1=s1, scalar2=s2, op0=op0, op1=op1),
                 reads=[a_t, prm, dvc] + list(extra_reads), writes=[out_t])

    def stt(out_t, out_ap, a_t, a_ap, scalar, b_t, b_ap, op0, op1, extra_reads=()):
        P.op("dve", lambda e: e.scalar_tensor_tensor(out=out_ap, in0=a_ap, scalar=scalar, in1=b_ap, op0=op0, op1=op1),
             reads=[a_t, b_t, prm, dvc] + list(extra_reads), writes=[out_t])

    def cp(eng, out_t, out_ap, in_t, in_ap):
        P.op(eng, lambda e: e.tensor_copy(out=out_ap, in_=in_ap), reads=[in_t], writes=[out_t])

    def memset(eng, t, ap, v):
        P.op(eng, lambda e: e.memset(ap, v), writes=[t])

    def rstd_from_psum(pt, n, inv_count, out_t):
        act(out_t, out_t.ap[:, 0:n], pt, pt.ap[:, 0:n], AF.Ln, bias=epsb.ap[:, 0:1], scale=inv_count, extra_reads=[epsb])
        act(out_t, out_t.ap[:, 0:n], out_t, out_t.ap[:, 0:n], AF.Exp, scale=-0.5)

    NSTG = 2
    wstage_f = [tile(f"wsf{i}", [128, 512], F32) for i in range(NSTG)]
    wstage_b = [tile(f"wsb{i}", [128, 512], BF16) for i in range(NSTG)]
    cstf = wstage_f[0]
    iota_f = tf("iota")
    yg = yown
    epsb = tile("epsb", [128, 3], F32)
    memset("pool", epsb, epsb.ap[:, 0:1], EPS)
    memset("pool", epsb, epsb.ap[:, 1:2], 1.0)
    memset("pool", epsb, epsb.ap[:, 2:3], 1e-30)
    dma(prm, prm.ap[:, :], params_d)
    dma(cstf, cstf.ap[:, 0:256], cst_d)
    memset("pool", ones_bf, ones_bf.ap[:, :], 1.0)
    memset("pool", ones_f, ones_f.ap[:, :], 1.0)
    memset("pool", bones_bf, bones_bf.ap[:, :], 0.0)
    memset("pool", bones_bf, bones_bf.ap[0:64, 0:64], 1.0)
    memset("pool", bones_bf, bones_bf.ap[64:128, 64:128], 1.0)
    for c in range(4):
        memset("pool", hbuf[c], hbuf[c].ap[:, :], 0.0)
        memset("pool", xr[c], xr[c].ap[:, :], 0.0)
        memset("pool", yown[c], yown[c].ap[:, :], 0.0)
    P.op("pool", lambda e: e.iota(ti32.ap, pattern=[[1, W]], base=0, channel_multiplier=0), writes=[ti32])
    cp("dve", iota_f, iota_f.ap[:, :], ti32, ti32.ap)
    cp("dve", cstb, cstb.ap[:, 0, :], cstf, cstf.ap[:, 0:128])
    cp("dve", cstb, cstb.ap[:, 1, :], cstf, cstf.ap[:, 128:256])
    for i2 in range(2):
        wgs = wstage_f[1]
        dma(wgs, wgs.ap[:, :], wg_d[:, i2 * 512:(i2 + 1) * 512])
        for i in range(4):
            cp("pool", wgb, wgb.ap[:, i2 * 4 + i, :], wgs, wgs.ap[:, i * 128:(i + 1) * 128])
    for i in range(9):
        st = xs[i % 8]
        dma(st, st.ap[:, :], mask_d[:, i * W:(i + 1) * W])
        cp("pool", maskb, maskb.ap[:, i, :], st, st.ap[:, :])

    def sincos_turns(y_t, n, out_sin_t, out_sin_ap, out_cos_t, out_cos_ap):
        r = tf("sc_r")
        nf = tf("sc_nf")
        for (off, o_t, o_ap) in ((0.0, out_sin_t, out_sin_ap), (0.25, out_cos_t, out_cos_ap)):
            ts("dve", r, r.ap[:, 0:n], y_t, y_t.ap[:, 0:n], off, None, ALU.add)
            cp("dve", ti32, ti32.ap[:, 0:n], r, r.ap[:, 0:n])
            cp("dve", nf, nf.ap[:, 0:n], ti32, ti32.ap[:, 0:n])
            tt("dve", r, r.ap[:, 0:n], r, r.ap[:, 0:n], nf, nf.ap[:, 0:n], ALU.subtract)
            act(o_t, o_ap, r, r.ap[:, 0:n], AF.Sin, scale=6.283185)

    yv = tf("sc_y")
    ts("dve", yv, yv.ap[:, :], iota_f, iota_f.ap[:, :], pcol("fturn"), None, ALU.mult)
    sincos_turns(yv, W, dS, dS.ap[:, :], dC, dC.ap[:, :])
    ts("dve", yv, yv.ap[:, 0:32], prm, pcol("kpos", 0, 32), pcol("fturn"), None, ALU.mult)
    ts("dve", yv, yv.ap[:, 32:40], prm, pcol("qpos", 0, 8), pcol("fturn"), None, ALU.mult)
    sincos_turns(yv, 40, c0s0, c0s0.ap[:, 40:80], c0s0, c0s0.ap[:, 0:40])
    t0 = tf("su0")
    act(t0, t0.ap[:, 0:4], prm, pcol("lru", 0, 4), AF.Exp, scale=-1.0)
    act(t0, t0.ap[:, 0:4], t0, t0.ap[:, 0:4], AF.Ln, bias=epsb.ap[:, 1:2], extra_reads=[epsb])
    ts("dve", dvc, dvc.ap[:, 0:4], t0, t0.ap[:, 0:4], -8.0, None, ALU.mult)
    ts("dve", dvc, dvc.ap[:, 4:8], t0, t0.ap[:, 0:4], -16.0, None, ALU.mult)
    ts("dve", dvc, dvc.ap[:, 8:9], prm, pcol("qw"), 0.125, None, ALU.mult)
    ts("dve", dvc, dvc.ap[:, 10:11], prm, pcol("subw"), 1.0 - LAMBDA_INIT, None, ALU.mult)
    lo, _ = PC["lam"]
    t1 = tf("su1")
    tt("dve", t1, t1.ap[0:64, 0:1], prm, prm.ap[0:64, lo:lo + 1], prm, prm.ap[0:64, lo + 1:lo + 2], ALU.mult)
    tt("dve", t1, t1.ap[0:64, 1:2], prm, prm.ap[0:64, lo + 2:lo + 3], prm, prm.ap[0:64, lo + 3:lo + 4], ALU.mult)
    mm(ps[0], ps[0].ap[:, 0:2], ones_f, ones_f.ap[0:64, :], t1, t1.ap[0:64, 0:2], True, True)
    act(t0, t0.ap[:, 0:2], ps[0], ps[0].ap[:, 0:2], AF.Exp)
    tt("dve", t0, t0.ap[:, 2:3], t0, t0.ap[:, 1:2], t0, t0.ap[:, 0:1], ALU.subtract)
    ts("dve", dvc, dvc.ap[:, 9:10], t0, t0.ap[:, 2:3], -LAMBDA_INIT, None, ALU.add)

    dvb = tile("dvb", [128, 8], F32)
    ts("dve", dvb, dvb.ap[:, 0:4], prm, pcol("bga", 0, 4), -1.0, None, ALU.mult)
    ts("dve", dvb, dvb.ap[:, 4:8], prm, pcol("bgx", 0, 4), -1.0, None, ALU.mult)
    for i in range(16):
        o_, _w = PC["crw"]
        ts("dve", dgb, dgb.ap[:, i, :], cstf, cstf.ap[:, 128:256], prm.ap[:, o_ + i:o_ + i + 1], None, ALU.mult)

    stg_i = [0]
    win_t = T(None, "win_s")
    wout_t = T(None, "wout_s")
    wup_t = T(None, "wup_s")
    wdn_t = T(None, "wdn_s")

    def prep(src, rows_k, ncols, scale_fn, dst_t, dst, kdim, engs=("pool",)):
        pieces = [(k, c0, min(512, ncols - c0)) for k in range(rows_k) for c0 in range(0, ncols, 512)]
        npc = len(pieces)
        base = stg_i[0]
        stg_i[0] += npc

        def bufs(j):
            i = (base + j) % NSTG
            return wstage_f[i], wstage_b[i]

        for s_ in range(npc + 2):
            if s_ < npc:
                k, c0, cw = pieces[s_]
                sf, sbb = bufs(s_)
                dma(sf, sf.ap[:, 0:cw], src[k * 128:(k + 1) * 128, c0:c0 + cw])
            j = s_ - 1
            if 0 <= j < npc:
                k, c0, cw = pieces[j]
                sf, sbb = bufs(j)
                sc = scale_fn(k)
                eng_ = engs[j % len(engs)]
                if sc is None:
                    cp(eng_, sbb, sbb.ap[:, 0:cw], sf, sf.ap[:, 0:cw])
                elif eng_ == "act":
                    act(sbb, sbb.ap[:, 0:cw], sf, sf.ap[:, 0:cw], AF.Copy, scale=sc)
                else:
                    ts(eng_, sbb, sbb.ap[:, 0:cw], sf, sf.ap[:, 0:cw], sc, 1.0, ALU.mult, ALU.mult)
            j = s_ - 2
            if 0 <= j < npc:
                k, c0, cw = pieces[j]
                sf, sbb = bufs(j)
                nchk = cw // 128
                dst_ap = dst[c0 // 128:c0 // 128 + nchk, :, k * 128:(k + 1) * 128].rearrange("c p x -> p c x")
                dma(dst_t, dst_ap, sbb.ap[:, 0:cw].rearrange("p (c x) -> p c x", x=128), reads=[sbb], nowaw=True)
            yield

    for _ in prep(w_in, 8, 2560, lambda k: pcol("n1w", k), win_t, win_s, 8, engs=("pool", "dve", "act")):
        pass

    def wout_scale(k):
        return dvc.ap[:, 10:11] if k < 4 else pcol("rnw", k - 4)

    def run_chains(gens):
        live = list(gens)
        while live:
            nxt = []
            for g in live:
                try:
                    next(g)
                    nxt.append(g)
                except StopIteration:
                    pass
            live = nxt

    def wout_scale(k):
        return dvc.ap[:, 10:11] if k < 4 else pcol("rnw", k - 4)

    def prep_all_bg():
        yield from prep(w_out, 8, 1024, wout_scale, wout_t, wout_s, 8, engs=("dve",))
        yield from prep(w_up, 8, 6144, lambda k: pcol("n2w", k), wup_t, wup_s, 8, engs=("dve",))
        yield from prep(w_down, 24, 1024, lambda k: None, wdn_t, wdn_s, 24, engs=("dve",))

    bg = prep_all_bg()

    def bg_chain(npieces):
        for _ in range(npieces):
            try:
                next(bg)
            except StopIteration:
                return
            yield

    wfree = list(wst)

    def acquire():
        return wfree.pop(0) if wfree else None

    def release(w):
        wfree.append(w)

    def load_w(dst_t, chunk, src_t, src):
        dma(dst_t, dst_t.ap[:, :, :], src[chunk].rearrange("p (k x) -> p k x", x=128), reads=[src_t])

    def stage0(src_cols, n, xb, stat_bank):
        for k in range(8):
            dma(xs[k], xs[k].ap[:, 0:n], src_cols(k), q="sp")
        yield
        for k in range(8):
            q = xsq[k % 2]
            act(q, q.ap[:, 0:n], xs[k], xs[k].ap[:, 0:n], AF.Square)
            mm(stat_bank, stat_bank.ap[:, 0:n], ones_bf, ones_bf.ap[:, :], q, q.ap[:, 0:n], k == 0, k == 7)
            if k % 2:
                yield
        rstd_from_psum(stat_bank, n, 1.0 / 1024.0, rstd_t)
        yield
        for k in range(8):
            tt("pool" if k % 4 == 3 else "dve", xb[k], xb[k].ap[:, 0:n], xs[k], xs[k].ap[:, 0:n],
               rstd_t, rstd_t.ap[:, 0:n], ALU.mult)
            if k % 2:
                yield

    def gen_tables(col, n, C, Sn):
        c0 = c0s0.ap[:, col:col + 1]
        s0 = c0s0.ap[:, 40 + col:41 + col]
        ts("pool", rTa, rTa.ap[:, 0:n], dC, dC.ap[:, 0:n], c0, 1.0, ALU.mult, ALU.mult, extra_reads=[c0s0])
        ts("pool", rTb, rTb.ap[:, 0:n], dS, dS.ap[:, 0:n], s0, 1.0, ALU.mult, ALU.mult, extra_reads=[c0s0])
        yield
        tt("pool", C, C.ap[:, 0:n], rTa, rTa.ap[:, 0:n], rTb, rTb.ap[:, 0:n], ALU.subtract)
        yield
        ts("pool", rTa, rTa.ap[:, 0:n], dC, dC.ap[:, 0:n], s0, 1.0, ALU.mult, ALU.mult, extra_reads=[c0s0])
        ts("pool", rTb, rTb.ap[:, 0:n], dS, dS.ap[:, 0:n], c0, 1.0, ALU.mult, ALU.mult, extra_reads=[c0s0])
        yield
        tt("pool", Sn, Sn.ap[:, 0:n], rTa, rTa.ap[:, 0:n], rTb, rTb.ap[:, 0:n], ALU.add)

    def qk_chain(h, wchunk, xb, n, wcol, C, Sn, out_t, out_ap, banks=None):
        tmp = kc[h]
        rk, kn, ksq, knb = tmp["rk"], tmp["kn"], tmp["ksq"], tmp["knb"]
        bank = ps[h] if banks is None else banks[0]
        w = acquire()
        while w is None:
            yield
            w = acquire()
        load_w(w, wchunk, win_t, win_s)
        yield
        for k in range(8):
            mm(bank, bank.ap[:, 0:n], w, w.ap[:, k, :], xb[k], xb[k].ap[:, 0:n], k == 0, k == 7)
        release(w)
        yield
        act(ksq, ksq.ap[:, 0:n], bank, bank.ap[:, 0:n], AF.Square)
        yield
        sb2 = bank.ap[:, 256:256 + n] if n <= 256 else None
        if sb2 is None:
            side = ps[4 + h] if banks is None else banks[1]
            ssq_ap = side.ap[:, 0:n]
        else:
            side = bank
            ssq_ap = sb2
        mm(side, ssq_ap, bones_bf, bones_bf.ap[:, :], ksq, ksq.ap[:, 0:n], True, True)
        yield
        act(rk, rk.ap[:, 0:n], side, ssq_ap, AF.Ln, bias=epsb.ap[:, 0:1], scale=1.0 / 64.0, extra_reads=[epsb])
        yield
        act(rk, rk.ap[:, 0:n], rk, rk.ap[:, 0:n], AF.Exp, scale=-0.5)
        yield
        stt(knb, knb.ap[:, 0:n], bank, bank.ap[:, 0:n], wcol, rk, rk.ap[:, 0:n], ALU.mult, ALU.mult)
        yield
        mm(side, ssq_ap, cstb, cstb.ap[:, 0, :], knb, knb.ap[:, 0:n], True, True)
        yield
        tt("dve", kn, kn.ap[:, 0:n], knb, knb.ap[:, 0:n], C, C.ap[:, 0:n], ALU.mult)
        yield
        tt("dve", rk, rk.ap[:, 0:n], side, ssq_ap, Sn, Sn.ap[:, 0:n], ALU.mult)
        yield
        if isinstance(out_t, tuple):
            qa, qb = out_t
            memset("pool", qa, qa.ap[64:128, 0:n], 0.0)
            memset("pool", qb, qb.ap[0:64, 0:n], 0.0)
            tt("dve", qa, qa.ap[0:64, 0:n], kn, kn.ap[0:64, 0:n], rk, rk.ap[0:64, 0:n], ALU.add)
            tt("dve", qb, qb.ap[64:128, 0:n], kn, kn.ap[64:128, 0:n], rk, rk.ap[64:128, 0:n], ALU.add)
        else:
            tt("dve", out_t, out_ap, kn, kn.ap[:, 0:n], rk, rk.ap[:, 0:n], ALU.add)

    def v_chain(c, xb):
        pv = ps[4]
        for hp in range(2):
            for hh in range(2):
                h = hp * 2 + hh
                w = acquire()
                while w is None:
                    yield
                    w = acquire()
                load_w(w, 8 + h, win_t, win_s)
                yield
                for tbk in range(2):
                    blk = tbk * 2 + hh
                    for k in range(8):
                        mm(pv, pv.ap[:, blk * 128:(blk + 1) * 128], xb[k], xb[k].ap[:, tbk * 128:(tbk + 1) * 128],
                           w, w.ap[:, k, :], k == 0, k == 7)
                    yield
                release(w)
            for tbk in range(2):
                vt = VVt[c * 2 + tbk]
                cp("dve", vt, VV[:, c * 2 + tbk, hp * 256:(hp + 1) * 256], pv, pv.ap[:, tbk * 256:(tbk + 1) * 256])
            yield

    xcb2 = [tile(f"xcb2_{r}", [128, W], BF16) for r in range(2)]

    def rnn_front(ch, xb, bank, xcb_t, xcb_ap):
        w = acquire()
        while w is None:
            yield
            w = acquire()
        load_w(w, 12 + ch, win_t, win_s)
        yield
        for k in range(8):
            mm(bank, bank.ap[:, 0:CW], w, w.ap[:, k, :], xb[k], xb[k].ap[:, 0:CW], k == 0, k == 7)
        release(w)
        yield
        X = xr[ch]
        cp("dve", X, X.ap[:, 3:259], bank, bank.ap[:, 0:CW])
        yield
        for tap in range(4):
            mm(bank, bank.ap[:, 256:512], dgb, dgb.ap[:, ch * 4 + tap, :], X, X.ap[:, tap:tap + CW], tap == 0, tap == 3)
        yield
        ts("dve", xcb_t, xcb_ap, bank, bank.ap[:, 256:512], pcol("crb", ch), None, ALU.add)
        yield
        cp("pool", X, X.ap[:, 0:3], X, X.ap[:, 256:259])
        mm(bank, bank.ap[:, 0:CW], wgb, wgb.ap[:, ch, :], xcb_t, xcb_ap, True, True)
        mm(bank, bank.ap[:, 256:512], wgb, wgb.ap[:, 4 + ch, :], xcb_t, xcb_ap, True, True)
        yield

    def rnn_back(ch, m, bank, tmp, xcb_t, xcb_ap):
        rr, ig, aa = tmp["r"], tmp["ig"], tmp["a"]
        act(rr, rr.ap[:, 0:CW], bank, bank.ap[:, 0:CW], AF.Exp, scale=-1.0, bias=dvb.ap[:, ch:ch + 1], extra_reads=[dvb])
        act(ig, ig.ap[:, 0:CW], bank, bank.ap[:, 256:512], AF.Exp, scale=-1.0, bias=dvb.ap[:, 4 + ch:5 + ch], extra_reads=[dvb])
        yield
        act(rr, rr.ap[:, 0:CW], rr, rr.ap[:, 0:CW], AF.Ln, bias=epsb.ap[:, 1:2], extra_reads=[epsb])
        yield
        act(rr, rr.ap[:, 0:CW], rr, rr.ap[:, 0:CW], AF.Exp, scale=-1.0)
        yield
        act(ig, ig.ap[:, 0:CW], ig, ig.ap[:, 0:CW], AF.Ln, bias=epsb.ap[:, 1:2], extra_reads=[epsb])
        yield
        act(ig, ig.ap[:, 0:CW], ig, ig.ap[:, 0:CW], AF.Exp, scale=-1.0)
        yield
        act(aa, aa.ap[:, 0:CW], rr, rr.ap[:, 0:CW], AF.Exp, scale=dvc.ap[:, ch:ch + 1])
        yield
        tt("dve", rr, rr.ap[:, 0:CW], aa, aa.ap[:, 0:CW], aa, aa.ap[:, 0:CW], ALU.mult)
        yield
        act(rr, rr.ap[:, 0:CW], rr, rr.ap[:, 0:CW], AF.Ln, scale=-1.0, bias=epsb.ap[:, 1:2], extra_reads=[epsb])
        yield
        act(rr, rr.ap[:, 0:CW], rr, rr.ap[:, 0:CW], AF.Exp, scale=0.5)
        yield
        tt("dve", ig, ig.ap[:, 0:CW], ig, ig.ap[:, 0:CW], rr, rr.ap[:, 0:CW], ALU.mult)
        yield
        tt("dve", ig, ig.ap[:, 0:CW], ig, ig.ap[:, 0:CW], xcb_t, xcb_ap, ALU.mult)
        yield
        H = hbuf[ch]
        cp("pool", H, H.ap[:, 0:2], H, H.ap[:, 256:258])
        yield
        P.op("dve", lambda e, H=H, aa=aa, ig=ig: e.tensor_tensor_scan(
            out=H.ap[:, 2:258], data0=aa.ap[:, 0:CW], data1=ig.ap[:, 0:CW], initial=H.ap[:, 1:2],
            op0=ALU.mult, op1=ALU.add), reads=[aa, ig, H], writes=[H])
        yield
        Y = yown[ch]
        if m == 0:
            ts("dve", Y, Y.ap[:, :], H, H.ap[:, :], pcol("sel", 0), None, ALU.mult)
        else:
            stt(Y, Y.ap[:, :], H, H.ap[:, :], pcol("sel", m), Y, Y.ap[:, :], ALU.mult, ALU.add)
        yield

    def rnn_chain(r, m, xb):
        tmp = rc[r]
        bank = ps[6 + r]
        xa_t, xa_ap = tmp["xcb"], tmp["xcb"].ap[:, 0:CW]
        xb_t = xcb2[r]
        xb_ap = xcb2[r].ap[:, 0:CW]
        yield from rnn_front(r, xb, bank, xa_t, xa_ap)
        g1 = rnn_back(r, m, bank, tmp, xa_t, xa_ap)
        g2 = rnn_front(r + 2, xb, bank, xb_t, xb_ap)
        next(g1)
        yield
        live = [g1, g2]
        while live:
            nxt = []
            for g in live:
                try:
                    next(g)
                    nxt.append(g)
                except StopIteration:
                    pass
            live = nxt
            yield
        yield from rnn_back(r + 2, m, bank, tmp, xb_t, xb_ap)

    def a_src(c):
        return lambda k: xT[k * 128:(k + 1) * 128, c * CW:(c + 1) * CW]

    def seq(*gens):
        for g in gens:
            yield from g

    def flagged(gen, flag):
        yield from gen
        flag[0] = True

    k_done = [0]
    q_done = [0]
    bt_ready = [False]

    def q_tail(h, parb):
        k_done[0] += 1
        while not (b_ready[0] and bt_ready[0]) or k_done[0] < 4:
            yield
        if h >= 2:
            while q_done[0] < 2:
                yield
        yield from qk_chain(h, h, xbp[parb], W, dvc.ap[:, 8:9], Ct[parb], St[parb],
                            (kc[h]["ksq"], kc[h]["knb"]), None, banks=(ps[h % 2], ps[2 + h % 2]))
        q_done[0] += 1

    def phase_a_chains(c):
        par = c % 2
        xb = xbp[par]
        ch = [qk_chain(h, 4 + h, xb, CW, pcol("kw"), Ct[par], St[par], KTt[h][c], KT[:, h, c * CW:(c + 1) * CW])
              for h in range(4)]
        if c % 4 == 3:
            k_done[0] = 0
            q_done[0] = 0
            ch = [seq(ch[h], q_tail(h, (c + 1) % 2)) for h in range(4)]
        ch.append(v_chain(c, xb))
        if c % 4 == 3:
            xbo = xbp[(c + 1) % 2]
            ch.append(seq(rnn_chain(0, c % 4, xb), gr_chain(0, xbo, None)))
            ch.append(seq(rnn_chain(1, c % 4, xb), gr_chain(1, xbo, None)))
        else:
            ch.append(rnn_chain(0, c % 4, xb))
            ch.append(rnn_chain(1, c % 4, xb))
        ch.append(bg_chain(6))
        return ch

    xmid_t = [T(None, f"xmid{n}") for n in range(NSLOT)]

    b_ready = [False]

    def gr_chain(r, xb, mixl):
        while not b_ready[0]:
            yield
        tmp = rc[r]
        gl = tmp["r"]
        bank = ps[6 + r]
        for ch in (r, r + 2):
            w = acquire()
            while w is None:
                yield
                w = acquire()
            load_w(w, 16 + ch, win_t, win_s)
            yield
            for k in range(8):
                mm(bank, bank.ap[:, 0:W], w, w.ap[:, k, :], xb[k], xb[k].ap[:, 0:W], k == 0, k == 7)
            release(w)
            yield
            act(gl, gl.ap[:, :], bank, bank.ap[:, 0:W], AF.Gelu_apprx_tanh)
            yield
            tt("dve", yown[ch], yown[ch].ap[:, :], yown[ch], yown[ch].ap[:, :], gl, gl.ap[:, :], ALU.mult)
            yield

    def phase_b(n, par):
        xb = xbp[par]
        mixl = xbp[1 - par]
        C, Sn = Ct[par], St[par]
        QT = [kc[h]["ksq"] for h in range(4)]
        QTa = [kc[h]["ksq"] for h in range(4)]
        QTb = [kc[h]["knb"] for h in range(4)]
        ysq = xsq[0]
        for ch in range(4):
            act(ysq, ysq.ap[:, :], yown[ch], yown[ch].ap[:, :], AF.Square)
            mm(ps[4], ps[4].ap[:, 0:W], ones_bf, ones_bf.ap[:, :], ysq, ysq.ap[:, :], ch == 0, ch == 3)
        ry = rc[0]["a"]
        rstd_from_psum(ps[4], W, 1.0 / 512.0, ry)
        for ch in range(4):
            tt("dve", mixl[4 + ch], mixl[4 + ch].ap[:, :], yown[ch], yown[ch].ap[:, :], ry, ry.ap[:, :], ALU.mult)
        wo = []
        for dc in range(NWST - 1):
            w = acquire()
            assert w is not None
            load_w(w, dc, wout_t, wout_s)
            wo.append(w)
        nkt = 8 * n + 8
        steps = [(h, kt) for h in range(4) for kt in range(nkt)]
        pbuf = [(rc[0]["xcb"], rc[1]["xcb"]), (xsq[0], xsq[1])]

        def qk_mm(idx):
            h, kt = steps[idx]
            c = kt // 2
            ksl = slice(kt * 128, (kt + 1) * 128)
            sA, sB = ps[(idx % 2) * 2], ps[(idx % 2) * 2 + 1]
            mm(sA, sA.ap[:, 0:W], KTt[h][c], KT[:, h, ksl], QTa[h], QTa[h].ap[:, :], True, True)
            mm(sB, sB.ap[:, 0:W], KTt[h][c], KT[:, h, ksl], QTb[h], QTb[h].ap[:, :], True, True)

        qk_mm(0)
        for idx, (h, kt) in enumerate(steps):
            if idx + 1 < len(steps):
                qk_mm(idx + 1)
            sA, sB = ps[(idx % 2) * 2], ps[(idx % 2) * 2 + 1]
            p0, p1 = pbuf[idx % 2]
            act(p0, p0.ap[:, :], sA, sA.ap[:, 0:W], AF.Exp)
            act(p1, p1.ap[:, :], sB, sB.ap[:, 0:W], AF.Exp)
            mi = kt - (8 * n - 1)
            if mi >= 0:
                tt("dve", p0, p0.ap[:, :], p0, p0.ap[:, :], maskb, maskb.ap[:, mi, :], ALU.mult)
                tt("dve", p1, p1.ap[:, :], p1, p1.ap[:, :], maskb, maskb.ap[:, mi, :], ALU.mult)
            O0, O1, D0, D1 = ps[4], ps[5], ps[6], ps[7]
            vt = VVt[kt]
            vsl = VV[:, kt, h * 128:(h + 1) * 128]
            st, sp_ = kt == 0, kt == nkt - 1
            mm(O0, O0.ap[:, 0:W], vt, vsl, p0, p0.ap[:, :], st, sp_)
            mm(O1, O1.ap[:, 0:W], vt, vsl, p1, p1.ap[:, :], st, sp_)
            mm(D0, D0.ap[:, 0:W], ones_bf, ones_bf.ap[:, :], p0, p0.ap[:, :], st, sp_)
            mm(D1, D1.ap[:, 0:W], ones_bf, ones_bf.ap[:, :], p1, p1.ap[:, :], st, sp_)
            if kt == nkt - 1:
                hp = h % 2
                o0s, o1s, at, rs = rc[hp]["xc"], rc[hp]["r"], rc[hp]["ig"], rc[hp]["a"]
                rd0, rd1 = kc[hp]["rk"], kc[hp]["kn"]
                act(rd0, rd0.ap[:, :], D0, D0.ap[:, 0:W], AF.Ln, bias=epsb.ap[:, 2:3], extra_reads=[epsb])
                act(rd1, rd1.ap[:, :], D1, D1.ap[:, 0:W], AF.Ln, bias=epsb.ap[:, 2:3], extra_reads=[epsb])
                act(o0s, o0s.ap[:, :], O0, O0.ap[:, 0:W], AF.Copy)
                act(o1s, o1s.ap[:, :], O1, O1.ap[:, 0:W], AF.Copy)
                act(rd0, rd0.ap[:, :], rd0, rd0.ap[:, :], AF.Exp, scale=-1.0)
                act(rd1, rd1.ap[:, :], rd1, rd1.ap[:, :], AF.Exp, scale=-1.0)
                tt("dve", o0s, o0s.ap[:, :], o0s, o0s.ap[:, :], rd0, rd0.ap[:, :], ALU.mult)
                tt("dve", o1s, o1s.ap[:, :], o1s, o1s.ap[:, :], rd1, rd1.ap[:, :], ALU.mult)
                stt(mixl[h], mixl[h].ap[:, :], o1s, o1s.ap[:, :], dvc.ap[:, 9:10], o0s, o0s.ap[:, :], ALU.mult, ALU.add)
        for h in range(4):
            asq = pbuf[h % 2][h // 2]
            bk = ps[4 + h]
            rs = rc[h % 2]["a"] if h < 2 else rc[h % 2]["ig"]
            act(asq, asq.ap[:, :], mixl[h], mixl[h].ap[:, :], AF.Square)
            mm(bk, bk.ap[:, 0:W], ones_bf, ones_bf.ap[:, :], asq, asq.ap[:, :], True, True)
            rstd_from_psum(bk, W, 1.0 / 128.0, rs)
            tt("dve", mixl[h], mixl[h].ap[:, :], mixl[h], mixl[h].ap[:, :], rs, rs.ap[:, :], ALU.mult)
        for dc in range(8):
            w = wo[dc]
            po = ps[dc % 2]
            for k in range(8):
                mm(po, po.ap[:, 0:W], w, w.ap[:, k, :], mixl[k], mixl[k].ap[:, :], k == 0, k == 7)
            release(w)
            if len(wo) < 8:
                w2 = acquire()
                assert w2 is not None
                load_w(w2, len(wo), wout_t, wout_s)
                wo.append(w2)
            xm = rTa if dc % 2 else rTb
            tt("dve", xm, xm.ap[:, :], po, po.ap[:, 0:W], xs[dc], xs[dc].ap[:, 0:W], ALU.add)
            dma(xmid_t[n], xmid_s[n][:, dc * W:(dc + 1) * W], xm.ap[:, :], reads=[xm], nowaw=True)

    run_chains([stage0(a_src(0), CW, xbp[0], ps[5]), gen_tables(0, CW, Ct[0], St[0])])
    for c in range(NCH):
        chains = phase_a_chains(c)
        if c % 4 != 3 and c + 1 < NCH:
            chains.append(stage0(a_src(c + 1), CW, xbp[(c + 1) % 2], ps[5]))
            chains.append(gen_tables(c + 1, CW, Ct[(c + 1) % 2], St[(c + 1) % 2]))
        if c % 4 == 3:
            nb = c // 4
            b_ready[0] = False
            chains.append(flagged(stage0(lambda k, nb=nb: xoT[k * 128:(k + 1) * 128, nb * W:(nb + 1) * W], W,
                                         xbp[(c + 1) % 2], ps[5]), b_ready))
            bt_ready[0] = False
            chains.append(flagged(gen_tables(32 + nb, W, Ct[(c + 1) % 2], St[(c + 1) % 2]), bt_ready))
        run_chains(chains)
        if c % 4 == 3:
            phase_b(c // 4, (c + 1) % 2)
            if c + 1 < NCH:
                run_chains([stage0(a_src(c + 1), CW, xbp[(c + 1) % 2], ps[5]),
                            gen_tables(c + 1, CW, Ct[(c + 1) % 2], St[(c + 1) % 2])])
    for _ in bg:
        pass

    last = {e: (P.ins[e][-1] if P.ins[e] else None) for e in P.ENGS}

    def otile(ap, name):
        t = T(ap, name)
        t.readers = {e: i for e, i in last.items() if i is not None and e != "sp"}
        return t

    WDN = otile(arena[:, 0:24576].rearrange("p (k x) -> p k x", x=1024), "WDN")
    MT = [[otile(arena[:, 24576 + (g * 24 + ci) * 256:24576 + (g * 24 + ci + 1) * 256], f"mt{g}_{ci}")
           for ci in range(24)] for g in range(4)]
    H2 = [[otile(arena[:, 49152 + (g * 8 + k) * W:49152 + (g * 8 + k + 1) * W], f"h2{g}_{k}")
           for k in range(8)] for g in range(4)]
    out_t = T(None, "outT")
    for dc in range(8):
        dma(WDN, WDN.ap[:, :, dc * 128:(dc + 1) * 128], wdn_s[dc].rearrange("p (k x) -> p k x", x=128),
            reads=[wdn_t], nowaw=True)
    cfo, _ = PC["cfw"]
    cbo, _ = PC["cfb"]

    def ffn_chain(i, grp, cis):
        ur = [kc[2 * i]["rk"], kc[2 * i]["kn"]]
        cv = [kc[2 * i + 1]["rk"], kc[2 * i + 1]["kn"]]
        gl = rc[i]["xc"]
        cis = list(cis)
        nxt = None
        for ii, ci in enumerate(cis):
            if nxt is not None and len(nxt) == 2:
                wpair = nxt
            else:
                wpair = nxt or []
                for cc in (ci, 24 + ci)[len(wpair):]:
                    w = acquire()
                    while w is None:
                        yield
                        w = acquire()
                    load_w(w, cc, wup_t, wup_s)
                    wpair.append((cc, w))
            nxt = []
            if ii + 1 < len(cis):
                for cc in (cis[ii + 1], 24 + cis[ii + 1]):
                    w = acquire()
                    if w is None:
                        break
                    load_w(w, cc, wup_t, wup_s)
                    nxt.append((cc, w))
            yield
            for g in range(4):
                n = grp * 4 + g
                for hi, (cc, w) in enumerate(wpair):
                    pu = ps[2 * i + hi]
                    for k in range(8):
                        mm(pu, pu.ap[:, 0:W], w, w.ap[:, k, :], H2[g][k], H2[g][k].ap, k == 0, k == 7)
                    yield
                    wc = lambda tap, cc=cc: prm.ap[:, cfo + cc * 3 + tap:cfo + cc * 3 + tap + 1]
                    if n == 0:
                        act(ur[hi], ur[hi].ap[:, 0:257], pu, pu.ap[:, 0:257], AF.Copy)
                        act(cv[hi], cv[hi].ap[:, 0:CW], pu, pu.ap[:, 2:258], AF.Identity, scale=wc(2),
                            bias=prm.ap[:, cbo + cc:cbo + cc + 1])
                        ts("dve", ur[hi], ur[hi].ap[:, 0:2], ur[hi], ur[hi].ap[:, 0:2], pcol("hflag"), None, ALU.mult)
                        yield
                        stt(cv[hi], cv[hi].ap[:, 0:CW], ur[hi], ur[hi].ap[:, 1:257], wc(1), cv[hi], cv[hi].ap[:, 0:CW], ALU.mult, ALU.add)
                        yield
                        stt(cv[hi], cv[hi].ap[:, 0:CW], ur[hi], ur[hi].ap[:, 0:256], wc(0), cv[hi], cv[hi].ap[:, 0:CW], ALU.mult, ALU.add)
                        yield
                    else:
                        act(cv[hi], cv[hi].ap[:, 0:CW], pu, pu.ap[:, 2:258], AF.Identity, scale=wc(2),
                            bias=prm.ap[:, cbo + cc:cbo + cc + 1])
                        yield
                        stt(cv[hi], cv[hi].ap[:, 0:CW], pu, pu.ap[:, 1:257], wc(1), cv[hi], cv[hi].ap[:, 0:CW], ALU.mult, ALU.add)
                        yield
                        stt(cv[hi], cv[hi].ap[:, 0:CW], pu, pu.ap[:, 0:256], wc(0), cv[hi], cv[hi].ap[:, 0:CW], ALU.mult, ALU.add)
                        yield
                act(gl, gl.ap[:, 0:CW], cv[0], cv[0].ap[:, 0:CW], AF.Gelu_apprx_tanh)
                yield
                tt("pool", MT[g][ci], MT[g][ci].ap, gl, gl.ap[:, 0:CW], cv[1], cv[1].ap[:, 0:CW], ALU.mult)
                yield
            for _, w in wpair:
                release(w)

    def down_chain(i, grp):
        xm, ob = rc[i]["r"], rc[i]["ig"]
        pd = ps[6 + i]
        for g in range(4):
            n = grp * 4 + g
            for dc in range(i, 8, 2):
                dma(xm, xm.ap[:, :], xmid_s[n][:, dc * W:(dc + 1) * W], reads=[xmid_t[n]])
                for ci in range(24):
                    mm(pd, pd.ap[:, 0:CW], WDN, WDN.ap[:, ci, dc * 128:(dc + 1) * 128], MT[g][ci], MT[g][ci].ap, ci == 0, ci == 23)
                    if ci % 8 == 7:
                        yield
                tt("dve", ob, ob.ap[:, 0:CW], pd, pd.ap[:, 0:CW], xm, xm.ap[:, 2:258], ALU.add)
                yield
                dma(out_t, outT[dc * 128:(dc + 1) * 128, n * CW:(n + 1) * CW], ob.ap[:, 0:CW], reads=[ob], nowaw=True)
                yield

    for grp in range(2):
        for g in range(4):
            n = grp * 4 + g
            for k in range(8):
                dma(xs[k], xs[k].ap[:, 0:W], xmid_s[n][:, k * W:(k + 1) * W], reads=[xmid_t[n]])
            for k in range(8):
                q = xsq[k % 2]
                act(q, q.ap[:, 0:W], xs[k], xs[k].ap[:, 0:W], AF.Square)
                mm(ps[5], ps[5].ap[:, 0:W], ones_bf, ones_bf.ap[:, :], q, q.ap[:, 0:W], k == 0, k == 7)
            rstd_from_psum(ps[5], W, 1.0 / 1024.0, rstd_t)
            for k in range(8):
                tt("pool" if k % 4 == 3 else "dve", H2[g][k], H2[g][k].ap, xs[k], xs[k].ap[:, 0:W],
                   rstd_t, rstd_t.ap[:, 0:W], ALU.mult)
        run_chains([ffn_chain(0, grp, range(0, 24, 2)), ffn_chain(1, grp, range(1, 24, 2))])
        run_chains([down_chain(0, grp), down_chain(1, grp)])
    P.emit([out_t])
    return nc, es


_CACHE = {}


def _params_for(inp, j):
    p = np.zeros((128, NP), np.float32)

    def put(name, arr):
        o, w = PC[name]
        arr = np.asarray(arr, np.float32)
        assert arr.shape == (128, w), (name, arr.shape)
        p[:, o:o + w] = arr

    put("n1w", inp["norm1_w"][0].reshape(8, 128).T)
    put("qw", np.tile(inp["q_norm_w"][0], 2)[:, None])
    put("kw", np.tile(inp["k_norm_w"][0], 2)[:, None])
    lam = np.zeros((128, 4), np.float32)
    lam[:64, 0] = inp["lambda_q1"][0]
    lam[:64, 1] = inp["lambda_k1"][0]
    lam[:64, 2] = inp["lambda_q2"][0]
    lam[:64, 3] = inp["lambda_k2"][0]
    put("lam", lam)
    put("subw", inp["subln_w"][0][:, None])
    crw = inp["conv_rnn_w"][0]
    put("crw", crw.reshape(4, 4, 128).transpose(2, 1, 0).reshape(128, 16))
    put("crb", inp["conv_rnn_b"][0].reshape(4, 128).T)
    put("bga", inp["b_gate_a"][0].reshape(4, 128).T)
    put("bgx", inp["b_gate_x"][0].reshape(4, 128).T)
    put("lru", inp["lru_lambda"][0].reshape(4, 128).T)
    put("rnw", inp["rnn_norm_w"][0].reshape(4, 128).T)
    put("n2w", inp["norm2_w"][0].reshape(8, 128).T)
    cfw = inp["conv_ffn_w"][0]
    put("cfw", cfw.reshape(3, 48, 128).transpose(2, 1, 0).reshape(128, 144))
    put("cfb", inp["conv_ffn_b"][0].reshape(48, 128).T)
    sel = np.zeros((128, 4), np.float32)
    sel[:, j] = 1.0
    put("sel", sel)
    put("hflag", np.full((128, 1), 0.0 if j == 0 else 1.0, np.float32))
    put("qpos", np.tile((256.0 * (4 * np.arange(8) + j) - 2.0)[None, :], (128, 1)))
    put("kpos", np.tile((256.0 * np.arange(32))[None, :], (128, 1)))
    ft = np.zeros((128, 1), np.float32)
    for pp in range(128):
        i = pp % 64
        if i < 16:
            ft[pp, 0] = (500000.0 ** (-(2.0 * (i % 8)) / 16.0)) / (2.0 * np.pi)
    put("fturn", ft)
    return p


def _consts():
    rot = np.zeros((128, 128), np.float32)
    for blk in (0, 64):
        for i in range(8):
            rot[blk + i + 8, blk + i] = -1.0
            rot[blk + i, blk + i + 8] = 1.0
    return np.concatenate([rot, np.eye(128, dtype=np.float32)], axis=1)


def _mask(j):
    m = np.zeros((128, 9, W), np.float32)
    r = np.arange(128)[:, None]
    col = np.arange(W)[None, :]
    qrel = 256 * j - 2 + col
    for mi in range(9):
        krel = 128 * (mi - 1) + r
        m[:, mi, :] = (krel <= qrel).astype(np.float32)
    return m.reshape(128, 9 * W)


def kernel(**inp):
    x = np.asarray(inp["x"], np.float32)
    if "nc" not in _CACHE:
        _CACHE["nc"] = build()
    nc, _es = _CACHE["nc"]
    wg = np.zeros((128, 8, 128), np.float32)
    for gi, key in enumerate(("w_gate_a", "w_gate_x")):
        wgt = np.asarray(inp[key][0], np.float32)
        for ch in range(4):
            for hb in range(2):
                wg[hb * 64:(hb + 1) * 64, gi * 4 + ch, hb * 64:(hb + 1) * 64] = wgt[ch * 2 + hb]
    wg = wg.reshape(128, 1024)
    cst = _consts()
    in_maps = []
    for core in range(8):
        b, j = divmod(core, 4)
        xTb = np.ascontiguousarray(x[b].T)
        xo = np.zeros((D, NSLOT, W), np.float32)
        for n in range(NSLOT):
            s0 = 256 * (4 * n + j)
            if s0 >= 2:
                xo[:, n, :] = xTb[:, s0 - 2:s0 + 256]
            else:
                xo[:, n, 2:] = xTb[:, 0:256]
        in_maps.append({
            "xT": xTb, "xoT": xo.reshape(D, NSLOT * W),
            "w_in": np.ascontiguousarray(inp["w_in"][0], np.float32),
            "w_out": np.ascontiguousarray(inp["w_out"][0], np.float32),
            "w_up": np.ascontiguousarray(inp["w_up"][0], np.float32),
            "w_down": np.ascontiguousarray(inp["w_down"][0], np.float32),
            "params": _params_for(inp, j), "wg": wg, "cst": cst, "mask": _mask(j),
        })
    res = run_bass_kernel_spmd(nc, in_maps, core_ids=list(range(8)))
    out = np.zeros((2, S, D), np.float32)
    for core in range(8):
        b, j = divmod(core, 4)
        oT = np.asarray(res.results[core]["outT"]).reshape(D, NSLOT, CW)
        for n in range(NSLOT):
            s0 = 256 * (4 * n + j)
            out[b, s0:s0 + 256, :] = oT[:, n, :].T
    return out
```

```python
import math
from contextlib import ExitStack
import numpy as np
import concourse.bass as bass
import concourse.mybir as mybir
from concourse.bass_utils import run_bass_kernel_spmd

F32 = mybir.dt.float32
BF16 = mybir.dt.bfloat16
I32 = mybir.dt.int32
AF = mybir.ActivationFunctionType
ALU = mybir.AluOpType

D = 1024
S = 8192
NCH = 32
CW = 256
W = 258
NSLOT = 8
EPS = 1e-6
LAMBDA_INIT = 0.8 - 0.6 * math.exp(0.0)
SAME_SYNC = True
RAW_ONLY_SAME = True
EMBED_WAIT = True
N_A_TILES = 32

PC = {}
_o = 0
for _n, _w in [("n1w", 8), ("qw", 1), ("kw", 1), ("lam", 4), ("subw", 1), ("crw", 16), ("crb", 4),
               ("bga", 4), ("bgx", 4), ("lru", 4), ("rnw", 4), ("n2w", 8), ("cfw", 144), ("cfb", 48),
               ("sel", 4), ("hflag", 1), ("qpos", 8), ("kpos", 32), ("fturn", 1)]:
    PC[_n] = (_o, _w)
    _o += _w
NP = _o


class T:
    __slots__ = ("ap", "name", "last_w", "readers", "sem", "cnt")

    def __init__(self, ap, name):
        self.ap = ap
        self.name = name
        self.last_w = None
        self.readers = {}
        self.sem = None
        self.cnt = 0


class Ins:
    __slots__ = ("eng", "fn", "deps", "is_dma", "sem", "val", "signal", "sigval")

    def __init__(self, eng, fn):
        self.eng = eng
        self.fn = fn
        self.deps = []
        self.is_dma = False
        self.sem = None
        self.val = 0
        self.signal = False
        self.sigval = 0


class Prog:
    ENGS = ("pe", "act", "dve", "pool", "sp")

    def __init__(self, nc, es):
        self.nc = nc
        self.es = es
        self.ins = {e: [] for e in self.ENGS}
        self.nsem = 0

    def new_sem(self, name):
        self.nsem += 1
        return self.es.enter_context(self.nc.semaphore(name))

    def op(self, eng, fn, reads=(), writes=(), dma=False, nowaw=False):
        ins = Ins(eng, fn)
        ins.is_dma = dma
        deps = []
        raw = set()
        for t in reads:
            if t.last_w is not None:
                deps.append(t.last_w)
                raw.add(id(t.last_w))
        for t in writes:
            if t.last_w is not None and not (nowaw and t.last_w.is_dma and t.last_w.eng == eng):
                deps.append(t.last_w)
            deps.extend(t.readers.values())
        seen = set()
        for d in deps:
            if id(d) in seen or d is ins:
                continue
            seen.add(id(d))
            if not d.is_dma and d.eng == eng and (eng == "pe" or not SAME_SYNC):
                continue
            if RAW_ONLY_SAME and not d.is_dma and d.eng == eng and id(d) not in raw:
                continue
            ins.deps.append(d)
            d.signal = True
        if dma:
            t = writes[0]
            if t.sem is None:
                t.sem = self.new_sem("d_" + t.name)
            t.cnt += 1
            ins.sem = t.sem
            ins.val = 16 * t.cnt
        for t in reads:
            t.readers[eng if not dma else ("dma", id(ins))] = ins
        for t in writes:
            t.last_w = ins
            t.readers = {}
        self.ins[eng].append(ins)
        return ins

    def emit(self, final_waits):
        nc = self.nc
        esem = {e: self.es.enter_context(nc.semaphore("e_" + e)) for e in self.ENGS}
        for e in self.ENGS:
            c = 0
            for i in self.ins[e]:
                if i.signal and not i.is_dma:
                    c += 1
                    i.sigval = c
        block = self.es.enter_context(nc.Block())

        def run(e, eng):
            seen = {}
            for i in self.ins[e]:
                need = {}
                for d in i.deps:
                    if d.is_dma:
                        key, v, sem = ("dma", id(d.sem)), d.val, d.sem
                    else:
                        key, v, sem = d.eng, d.sigval, esem[d.eng]
                    if seen.get(key, 0) >= v:
                        continue
                    seen[key] = v
                    need[key] = (sem, v)
                waits = list(need.values())
                emb = None
                if waits and EMBED_WAIT:
                    emb = waits.pop()
                for sem, v in waits:
                    eng.wait_ge(sem, v)
                bi = i.fn(eng)
                if emb is not None:
                    bi = bi._wait_ge(emb[0], emb[1])
                if i.is_dma:
                    bi.then_inc(i.sem, 16)
                elif i.signal:
                    bi.then_inc(esem[e], 1)
            if e == "sp":
                for t in final_waits:
                    eng.wait_ge(t.sem, 16 * t.cnt)

        @block.tensor
        def _(eng):
            run("pe", eng)

        @block.scalar
        def _(eng):
            run("act", eng)

        @block.vector
        def _(eng):
            run("dve", eng)

        @block.gpsimd
        def _(eng):
            run("pool", eng)

        @block.sync
        def _(eng):
            run("sp", eng)


def build():
    nc = bass.Bass("TRN2", target_bir_lowering=False, dynamic_dma_scratch_size=512)
    es = ExitStack()
    P = Prog(nc, es)

    def din(name, shape, dt=F32):
        return nc.dram_tensor(name, shape, dt, kind="ExternalInput").ap()

    xT = din("xT", [D, S])
    xoT = din("xoT", [D, NSLOT * W])
    w_in = din("w_in", [D, 2560])
    w_out = din("w_out", [D, D])
    w_up = din("w_up", [D, 6144])
    w_down = din("w_down", [3072, D])
    params_d = din("params", [128, NP])
    wg_d = din("wg", [128, 8 * 128])
    cst_d = din("cst", [128, 2 * 128])
    mask_d = din("mask", [128, 9 * W])
    outT = nc.dram_tensor("outT", [D, NSLOT * CW], F32, kind="ExternalOutput").ap()
    win_s = nc.dram_tensor("win_s", [20, 128, 8 * 128], BF16).ap()
    wout_s = nc.dram_tensor("wout_s", [8, 128, 8 * 128], BF16).ap()
    wup_s = nc.dram_tensor("wup_s", [48, 128, 8 * 128], BF16).ap()
    wdn_s = nc.dram_tensor("wdn_s", [8, 128, 24 * 128], BF16).ap()
    xmid_s = nc.dram_tensor("xmid_s", [NSLOT, 128, 8 * W], F32).ap()

    def sb(name, shape, dt):
        return nc.alloc_sbuf_tensor(name, shape, dt)

    def tile(name, shape, dt):
        h = sb(name, shape, dt)
        return T(h, name)

    arena = sb("arena", [128, 65536], BF16)
    KT = arena[:, 0:32768].rearrange("p (h s) -> p h s", h=4)
    KTt = [[T(KT, f"KT{h}_{c}") for c in range(NCH)] for h in range(4)]
    VV = arena[:, 32768:65536].rearrange("p (b x) -> p b x", x=512)
    VVt = [T(VV, f"V{b}") for b in range(64)]
    prm = tile("prm", [128, NP], F32)
    wgb = tile("wgb", [128, 8, 128], BF16)
    cstb = tile("cstb", [128, 2, 128], BF16)
    maskb = tile("maskb", [128, 9, W], BF16)
    ones_bf = tile("ones_bf", [128, 128], BF16)
    bones_bf = tile("bones_bf", [128, 128], BF16)
    ones_f = tile("ones_f", [128, 128], F32)
    dC = tile("dC", [128, W], F32)
    dS = tile("dS", [128, W], F32)
    c0s0 = tile("c0s0", [128, 80], F32)
    dvc = tile("dvc", [128, 16], F32)
    hbuf = [tile(f"hbuf{c}", [128, W], F32) for c in range(4)]
    xr = [tile(f"xr{c}", [128, 260], BF16) for c in range(4)]
    yown = [tile(f"yown{c}", [128, W], F32) for c in range(4)]

    xs = [tile(f"xs{k}", [128, W], F32) for k in range(8)]
    xsq = [tile(f"xsq{i}", [128, W], BF16) for i in range(2)]
    xbp = [[tile(f"xb{p}_{k}", [128, W], BF16) for k in range(8)] for p in range(2)]
    Ct = [tile(f"Ct{p}", [128, W], F32) for p in range(2)]
    St = [tile(f"St{p}", [128, W], F32) for p in range(2)]
    rTa = tile("rTa", [128, W], F32)
    rTb = tile("rTb", [128, W], F32)
    rstd_t = tile("rstd", [128, W], F32)
    kc = [dict(rk=tile(f"k_rk{h}", [128, W], F32), kn=tile(f"k_kn{h}", [128, W], F32),
               ksq=tile(f"k_ksq{h}", [128, W], BF16), knb=tile(f"k_knb{h}", [128, W], BF16)) for h in range(4)]
    rc = [dict(xc=tile(f"r_xc{r}", [128, W], F32), r=tile(f"r_r{r}", [128, W], F32),
               ig=tile(f"r_ig{r}", [128, W], F32), a=tile(f"r_a{r}", [128, W], F32),
               xcb=tile(f"r_xcb{r}", [128, W], BF16)) for r in range(2)]
    NWST = 7
    wst = [tile(f"wst{i}", [128, 8, 128], BF16) for i in range(NWST)]
    dgb = tile("dgb", [128, 16, 128], BF16)
    ti32 = T(rTa.ap[:, :].bitcast(I32), "ti32")
    ft = {"kf": kc[0]["rk"], "rk": kc[1]["rk"], "kn": kc[0]["kn"], "kt1": kc[1]["kn"], "kt2": kc[2]["kn"],
          "rS": rTb}

    def tf(name):
        m = {"sc_r": "kf", "sc_nf": "rk", "sc_y": "kn", "su0": "kt1", "su1": "kt2", "iota": "rS"}
        return ft[m.get(name, name)]

    ps = [T(es.enter_context(nc.psum_tensor(f"ps{i}", [128, 512], F32)), f"ps{i}") for i in range(8)]

    def pcol(name, i=0, n=1):
        o, w = PC[name]
        return prm.ap[:, o + i:o + i + n]

    wst_i = [0]

    def next_wst():
        t = wst[wst_i[0] % NWST]
        wst_i[0] += 1
        return t

    def dma(out_t, out_ap, in_ap, reads=(), nowaw=False, q="sp"):
        P.op(q, lambda e: e.dma_start(out=out_ap, in_=in_ap), reads=reads, writes=[out_t], dma=True, nowaw=nowaw)

    def act(out_t, out_ap, in_t, in_ap, func, bias=None, scale=None, extra_reads=()):
        kw = {}
        if bias is not None:
            kw["bias"] = bias
        if scale is not None:
            kw["scale"] = scale
        P.op("act", lambda e: e.activation(out=out_ap, in_=in_ap, func=func, **kw),
             reads=[in_t, prm, dvc] + list(extra_reads), writes=[out_t])

    def mm(out_t, out_ap, lt, l_ap, rt, r_ap, start, stop):
        P.op("pe", lambda e: e.matmul(out_ap, l_ap, r_ap, start=start, stop=stop),
             reads=[lt, rt], writes=[out_t])

    def tt(eng, out_t, out_ap, a_t, a_ap, b_t, b_ap, op):
        P.op(eng, lambda e: e.tensor_tensor(out=out_ap, in0=a_ap, in1=b_ap, op=op),
             reads=[a_t, b_t], writes=[out_t])

    def ts(eng, out_t, out_ap, a_t, a_ap, s1, s2, op0, op1=None, extra_reads=()):
        if op1 is None:
            P.op(eng, lambda e: e.tensor_scalar(out=out_ap, in0=a_ap, scalar1=s1, scalar2=None, op0=op0),
                 reads=[a_t, prm, dvc] + list(extra_reads), writes=[out_t])
        else:
            P.op(eng, lambda e: e.tensor_scalar(out=out_ap, in0=a_ap, scalar1=s1, scalar2=s2, op0=op0, op1=op1),
                 reads=[a_t, prm, dvc] + list(extra_reads), writes=[out_t])

    def stt(out_t, out_ap, a_t, a_ap, scalar, b_t, b_ap, op0, op1, extra_reads=()):
        P.op("dve", lambda e: e.scalar_tensor_tensor(out=out_ap, in0=a_ap, scalar=scalar, in1=b_ap, op0=op0, op1=op1),
             reads=[a_t, b_t, prm, dvc] + list(extra_reads), writes=[out_t])

    def cp(eng, out_t, out_ap, in_t, in_ap):
        P.op(eng, lambda e: e.tensor_copy(out=out_ap, in_=in_ap), reads=[in_t], writes=[out_t])

    def memset(eng, t, ap, v):
        P.op(eng, lambda e: e.memset(ap, v), writes=[t])

    def rstd_from_psum(pt, n, inv_count, out_t):
        act(out_t, out_t.ap[:, 0:n], pt, pt.ap[:, 0:n], AF.Ln, bias=epsb.ap[:, 0:1], scale=inv_count, extra_reads=[epsb])
        act(out_t, out_t.ap[:, 0:n], out_t, out_t.ap[:, 0:n], AF.Exp, scale=-0.5)

    NSTG = 2
    wstage_f = [tile(f"wsf{i}", [128, 512], F32) for i in range(NSTG)]
    wstage_b = [tile(f"wsb{i}", [128, 512], BF16) for i in range(NSTG)]
    cstf = wstage_f[0]
    iota_f = tf("iota")
    yg = yown
    epsb = tile("epsb", [128, 3], F32)
    memset("pool", epsb, epsb.ap[:, 0:1], EPS)
    memset("pool", epsb, epsb.ap[:, 1:2], 1.0)
    memset("pool", epsb, epsb.ap[:, 2:3], 1e-30)
    dma(prm, prm.ap[:, :], params_d)
    dma(cstf, cstf.ap[:, 0:256], cst_d)
    memset("pool", ones_bf, ones_bf.ap[:, :], 1.0)
    memset("pool", ones_f, ones_f.ap[:, :], 1.0)
    memset("pool", bones_bf, bones_bf.ap[:, :], 0.0)
    memset("pool", bones_bf, bones_bf.ap[0:64, 0:64], 1.0)
    memset("pool", bones_bf, bones_bf.ap[64:128, 64:128], 1.0)
    for c in range(4):
        memset("pool", hbuf[c], hbuf[c].ap[:, :], 0.0)
        memset("pool", xr[c], xr[c].ap[:, :], 0.0)
        memset("pool", yown[c], yown[c].ap[:, :], 0.0)
    P.op("pool", lambda e: e.iota(ti32.ap, pattern=[[1, W]], base=0, channel_multiplier=0), writes=[ti32])
    cp("dve", iota_f, iota_f.ap[:, :], ti32, ti32.ap)
    cp("dve", cstb, cstb.ap[:, 0, :], cstf, cstf.ap[:, 0:128])
    cp("dve", cstb, cstb.ap[:, 1, :], cstf, cstf.ap[:, 128:256])
    for i2 in range(2):
        wgs = wstage_f[1]
        dma(wgs, wgs.ap[:, :], wg_d[:, i2 * 512:(i2 + 1) * 512])
        for i in range(4):
            cp("pool", wgb, wgb.ap[:, i2 * 4 + i, :], wgs, wgs.ap[:, i * 128:(i + 1) * 128])
    for i in range(9):
        st = xs[i % 8]
        dma(st, st.ap[:, :], mask_d[:, i * W:(i + 1) * W])
        cp("pool", maskb, maskb.ap[:, i, :], st, st.ap[:, :])

    def sincos_turns(y_t, n, out_sin_t, out_sin_ap, out_cos_t, out_cos_ap):
        r = tf("sc_r")
        nf = tf("sc_nf")
        for (off, o_t, o_ap) in ((0.0, out_sin_t, out_sin_ap), (0.25, out_cos_t, out_cos_ap)):
            ts("dve", r, r.ap[:, 0:n], y_t, y_t.ap[:, 0:n], off, None, ALU.add)
            cp("dve", ti32, ti32.ap[:, 0:n], r, r.ap[:, 0:n])
            cp("dve", nf, nf.ap[:, 0:n], ti32, ti32.ap[:, 0:n])
            tt("dve", r, r.ap[:, 0:n], r, r.ap[:, 0:n], nf, nf.ap[:, 0:n], ALU.subtract)
            act(o_t, o_ap, r, r.ap[:, 0:n], AF.Sin, scale=6.283185)

    yv = tf("sc_y")
    ts("dve", yv, yv.ap[:, :], iota_f, iota_f.ap[:, :], pcol("fturn"), None, ALU.mult)
    sincos_turns(yv, W, dS, dS.ap[:, :], dC, dC.ap[:, :])
    ts("dve", yv, yv.ap[:, 0:32], prm, pcol("kpos", 0, 32), pcol("fturn"), None, ALU.mult)
    ts("dve", yv, yv.ap[:, 32:40], prm, pcol("qpos", 0, 8), pcol("fturn"), None, ALU.mult)
    sincos_turns(yv, 40, c0s0, c0s0.ap[:, 40:80], c0s0, c0s0.ap[:, 0:40])
    t0 = tf("su0")
    act(t0, t0.ap[:, 0:4], prm, pcol("lru", 0, 4), AF.Exp, scale=-1.0)
    act(t0, t0.ap[:, 0:4], t0, t0.ap[:, 0:4], AF.Ln, bias=epsb.ap[:, 1:2], extra_reads=[epsb])
    ts("dve", dvc, dvc.ap[:, 0:4], t0, t0.ap[:, 0:4], -8.0, None, ALU.mult)
    ts("dve", dvc, dvc.ap[:, 4:8], t0, t0.ap[:, 0:4], -16.0, None, ALU.mult)
    ts("dve", dvc, dvc.ap[:, 8:9], prm, pcol("qw"), 0.125, None, ALU.mult)
    ts("dve", dvc, dvc.ap[:, 10:11], prm, pcol("subw"), 1.0 - LAMBDA_INIT, None, ALU.mult)
    lo, _ = PC["lam"]
    t1 = tf("su1")
    tt("dve", t1, t1.ap[0:64, 0:1], prm, prm.ap[0:64, lo:lo + 1], prm, prm.ap[0:64, lo + 1:lo + 2], ALU.mult)
    tt("dve", t1, t1.ap[0:64, 1:2], prm, prm.ap[0:64, lo + 2:lo + 3], prm, prm.ap[0:64, lo + 3:lo + 4], ALU.mult)
    mm(ps[0], ps[0].ap[:, 0:2], ones_f, ones_f.ap[0:64, :], t1, t1.ap[0:64, 0:2], True, True)
    act(t0, t0.ap[:, 0:2], ps[0], ps[0].ap[:, 0:2], AF.Exp)
    tt("dve", t0, t0.ap[:, 2:3], t0, t0.ap[:, 1:2], t0, t0.ap[:, 0:1], ALU.subtract)
    ts("dve", dvc, dvc.ap[:, 9:10], t0, t0.ap[:, 2:3], -LAMBDA_INIT, None, ALU.add)

    dvb = tile("dvb", [128, 8], F32)
    ts("dve", dvb, dvb.ap[:, 0:4], prm, pcol("bga", 0, 4), -1.0, None, ALU.mult)
    ts("dve", dvb, dvb.ap[:, 4:8], prm, pcol("bgx", 0, 4), -1.0, None, ALU.mult)
    for i in range(16):
        o_, _w = PC["crw"]
        ts("dve", dgb, dgb.ap[:, i, :], cstf, cstf.ap[:, 128:256], prm.ap[:, o_ + i:o_ + i + 1], None, ALU.mult)

    stg_i = [0]
    win_t = T(None, "win_s")
    wout_t = T(None, "wout_s")
    wup_t = T(None, "wup_s")
    wdn_t = T(None, "wdn_s")

    def prep(src, rows_k, ncols, scale_fn, dst_t, dst, kdim, engs=("pool",)):
        pieces = [(k, c0, min(512, ncols - c0)) for k in range(rows_k) for c0 in range(0, ncols, 512)]
        npc = len(pieces)
        base = stg_i[0]
        stg_i[0] += npc

        def bufs(j):
            i = (base + j) % NSTG
            return wstage_f[i], wstage_b[i]

        for s_ in range(npc + 2):
            if s_ < npc:
                k, c0, cw = pieces[s_]
                sf, sbb = bufs(s_)
                dma(sf, sf.ap[:, 0:cw], src[k * 128:(k + 1) * 128, c0:c0 + cw])
            j = s_ - 1
            if 0 <= j < npc:
                k, c0, cw = pieces[j]
                sf, sbb = bufs(j)
                sc = scale_fn(k)
                eng_ = engs[j % len(engs)]
                if sc is None:
                    cp(eng_, sbb, sbb.ap[:, 0:cw], sf, sf.ap[:, 0:cw])
                elif eng_ == "act":
                    act(sbb, sbb.ap[:, 0:cw], sf, sf.ap[:, 0:cw], AF.Copy, scale=sc)
                else:
                    ts(eng_, sbb, sbb.ap[:, 0:cw], sf, sf.ap[:, 0:cw], sc, 1.0, ALU.mult, ALU.mult)
            j = s_ - 2
            if 0 <= j < npc:
                k, c0, cw = pieces[j]
                sf, sbb = bufs(j)
                nchk = cw // 128
                dst_ap = dst[c0 // 128:c0 // 128 + nchk, :, k * 128:(k + 1) * 128].rearrange("c p x -> p c x")
                dma(dst_t, dst_ap, sbb.ap[:, 0:cw].rearrange("p (c x) -> p c x", x=128), reads=[sbb], nowaw=True)
            yield

    for _ in prep(w_in, 8, 2560, lambda k: pcol("n1w", k), win_t, win_s, 8, engs=("pool", "dve", "act")):
        pass

    def wout_scale(k):
        return dvc.ap[:, 10:11] if k < 4 else pcol("rnw", k - 4)

    def run_chains(gens):
        live = list(gens)
        while live:
            nxt = []
            for g in live:
                try:
                    next(g)
                    nxt.append(g)
                except StopIteration:
                    pass
            live = nxt

    def wout_scale(k):
        return dvc.ap[:, 10:11] if k < 4 else pcol("rnw", k - 4)

    def prep_all_bg():
        yield from prep(w_out, 8, 1024, wout_scale, wout_t, wout_s, 8)
        yield from prep(w_up, 8, 6144, lambda k: pcol("n2w", k), wup_t, wup_s, 8)
        yield from prep(w_down, 24, 1024, lambda k: None, wdn_t, wdn_s, 24)

    bg = prep_all_bg()

    def bg_chain(npieces):
        for _ in range(npieces):
            try:
                next(bg)
            except StopIteration:
                return
            yield

    wfree = list(wst)

    def acquire():
        return wfree.pop(0) if wfree else None

    def release(w):
        wfree.append(w)

    def load_w(dst_t, chunk, src_t, src):
        dma(dst_t, dst_t.ap[:, :, :], src[chunk].rearrange("p (k x) -> p k x", x=128), reads=[src_t])

    def stage0(src_cols, n, xb, stat_bank):
        for k in range(8):
            dma(xs[k], xs[k].ap[:, 0:n], src_cols(k), q="sp")
        yield
        for k in range(8):
            q = xsq[k % 2]
            act(q, q.ap[:, 0:n], xs[k], xs[k].ap[:, 0:n], AF.Square)
            mm(stat_bank, stat_bank.ap[:, 0:n], ones_bf, ones_bf.ap[:, :], q, q.ap[:, 0:n], k == 0, k == 7)
            if k % 2:
                yield
        rstd_from_psum(stat_bank, n, 1.0 / 1024.0, rstd_t)
        yield
        for k in range(8):
            tt("pool" if k % 4 == 3 else "dve", xb[k], xb[k].ap[:, 0:n], xs[k], xs[k].ap[:, 0:n],
               rstd_t, rstd_t.ap[:, 0:n], ALU.mult)
            if k % 2:
                yield

    def gen_tables(col, n, C, Sn):
        c0 = c0s0.ap[:, col:col + 1]
        s0 = c0s0.ap[:, 40 + col:41 + col]
        ts("pool", rTa, rTa.ap[:, 0:n], dC, dC.ap[:, 0:n], c0, 1.0, ALU.mult, ALU.mult, extra_reads=[c0s0])
        ts("pool", rTb, rTb.ap[:, 0:n], dS, dS.ap[:, 0:n], s0, 1.0, ALU.mult, ALU.mult, extra_reads=[c0s0])
        yield
        tt("pool", C, C.ap[:, 0:n], rTa, rTa.ap[:, 0:n], rTb, rTb.ap[:, 0:n], ALU.subtract)
        yield
        ts("pool", rTa, rTa.ap[:, 0:n], dC, dC.ap[:, 0:n], s0, 1.0, ALU.mult, ALU.mult, extra_reads=[c0s0])
        ts("pool", rTb, rTb.ap[:, 0:n], dS, dS.ap[:, 0:n], c0, 1.0, ALU.mult, ALU.mult, extra_reads=[c0s0])
        yield
        tt("pool", Sn, Sn.ap[:, 0:n], rTa, rTa.ap[:, 0:n], rTb, rTb.ap[:, 0:n], ALU.add)

    def qk_chain(h, wchunk, xb, n, wcol, C, Sn, out_t, out_ap, banks=None):
        tmp = kc[h]
        rk, kn, ksq, knb = tmp["rk"], tmp["kn"], tmp["ksq"], tmp["knb"]
        bank = ps[h] if banks is None else banks[0]
        w = acquire()
        while w is None:
            yield
            w = acquire()
        load_w(w, wchunk, win_t, win_s)
        yield
        for k in range(8):
            mm(bank, bank.ap[:, 0:n], w, w.ap[:, k, :], xb[k], xb[k].ap[:, 0:n], k == 0, k == 7)
        release(w)
        yield
        act(ksq, ksq.ap[:, 0:n], bank, bank.ap[:, 0:n], AF.Square)
        yield
        sb2 = bank.ap[:, 256:256 + n] if n <= 256 else None
        if sb2 is None:
            side = ps[4 + h] if banks is None else banks[1]
            ssq_ap = side.ap[:, 0:n]
        else:
            side = bank
            ssq_ap = sb2
        mm(side, ssq_ap, bones_bf, bones_bf.ap[:, :], ksq, ksq.ap[:, 0:n], True, True)
        yield
        act(rk, rk.ap[:, 0:n], side, ssq_ap, AF.Ln, bias=epsb.ap[:, 0:1], scale=1.0 / 64.0, extra_reads=[epsb])
        yield
        act(rk, rk.ap[:, 0:n], rk, rk.ap[:, 0:n], AF.Exp, scale=-0.5)
        yield
        stt(knb, knb.ap[:, 0:n], bank, bank.ap[:, 0:n], wcol, rk, rk.ap[:, 0:n], ALU.mult, ALU.mult)
        yield
        mm(side, ssq_ap, cstb, cstb.ap[:, 0, :], knb, knb.ap[:, 0:n], True, True)
        yield
        tt("dve", kn, kn.ap[:, 0:n], knb, knb.ap[:, 0:n], C, C.ap[:, 0:n], ALU.mult)
        yield
        tt("dve", rk, rk.ap[:, 0:n], side, ssq_ap, Sn, Sn.ap[:, 0:n], ALU.mult)
        yield
        if isinstance(out_t, tuple):
            qa, qb = out_t
            memset("pool", qa, qa.ap[64:128, 0:n], 0.0)
            memset("pool", qb, qb.ap[0:64, 0:n], 0.0)
            tt("dve", qa, qa.ap[0:64, 0:n], kn, kn.ap[0:64, 0:n], rk, rk.ap[0:64, 0:n], ALU.add)
            tt("dve", qb, qb.ap[64:128, 0:n], kn, kn.ap[64:128, 0:n], rk, rk.ap[64:128, 0:n], ALU.add)
        else:
            tt("dve", out_t, out_ap, kn, kn.ap[:, 0:n], rk, rk.ap[:, 0:n], ALU.add)

    def v_chain(c, xb):
        pv = ps[4]
        for hp in range(2):
            for hh in range(2):
                h = hp * 2 + hh
                w = acquire()
                while w is None:
                    yield
                    w = acquire()
                load_w(w, 8 + h, win_t, win_s)
                yield
                for tbk in range(2):
                    blk = tbk * 2 + hh
                    for k in range(8):
                        mm(pv, pv.ap[:, blk * 128:(blk + 1) * 128], xb[k], xb[k].ap[:, tbk * 128:(tbk + 1) * 128],
                           w, w.ap[:, k, :], k == 0, k == 7)
                    yield
                release(w)
            for tbk in range(2):
                vt = VVt[c * 2 + tbk]
                cp("dve", vt, VV[:, c * 2 + tbk, hp * 256:(hp + 1) * 256], pv, pv.ap[:, tbk * 256:(tbk + 1) * 256])
            yield

    xcb2 = [tile(f"xcb2_{r}", [128, W], BF16) for r in range(2)]

    def rnn_front(ch, xb, bank, xcb_t, xcb_ap):
        w = acquire()
        while w is None:
            yield
            w = acquire()
        load_w(w, 12 + ch, win_t, win_s)
        yield
        for k in range(8):
            mm(bank, bank.ap[:, 0:CW], w, w.ap[:, k, :], xb[k], xb[k].ap[:, 0:CW], k == 0, k == 7)
        release(w)
        yield
        X = xr[ch]
        cp("dve", X, X.ap[:, 3:259], bank, bank.ap[:, 0:CW])
        yield
        for tap in range(4):
            mm(bank, bank.ap[:, 256:512], dgb, dgb.ap[:, ch * 4 + tap, :], X, X.ap[:, tap:tap + CW], tap == 0, tap == 3)
        yield
        ts("dve", xcb_t, xcb_ap, bank, bank.ap[:, 256:512], pcol("crb", ch), None, ALU.add)
        yield
        cp("pool", X, X.ap[:, 0:3], X, X.ap[:, 256:259])
        mm(bank, bank.ap[:, 0:CW], wgb, wgb.ap[:, ch, :], xcb_t, xcb_ap, True, True)
        mm(bank, bank.ap[:, 256:512], wgb, wgb.ap[:, 4 + ch, :], xcb_t, xcb_ap, True, True)
        yield

    def rnn_back(ch, m, bank, tmp, xcb_t, xcb_ap):
        rr, ig, aa = tmp["r"], tmp["ig"], tmp["a"]
        act(rr, rr.ap[:, 0:CW], bank, bank.ap[:, 0:CW], AF.Exp, scale=-1.0, bias=dvb.ap[:, ch:ch + 1], extra_reads=[dvb])
        act(ig, ig.ap[:, 0:CW], bank, bank.ap[:, 256:512], AF.Exp, scale=-1.0, bias=dvb.ap[:, 4 + ch:5 + ch], extra_reads=[dvb])
        yield
        act(rr, rr.ap[:, 0:CW], rr, rr.ap[:, 0:CW], AF.Ln, bias=epsb.ap[:, 1:2], extra_reads=[epsb])
        yield
        act(rr, rr.ap[:, 0:CW], rr, rr.ap[:, 0:CW], AF.Exp, scale=-1.0)
        yield
        act(ig, ig.ap[:, 0:CW], ig, ig.ap[:, 0:CW], AF.Ln, bias=epsb.ap[:, 1:2], extra_reads=[epsb])
        yield
        act(ig, ig.ap[:, 0:CW], ig, ig.ap[:, 0:CW], AF.Exp, scale=-1.0)
        yield
        act(aa, aa.ap[:, 0:CW], rr, rr.ap[:, 0:CW], AF.Exp, scale=dvc.ap[:, ch:ch + 1])
        yield
        tt("dve", rr, rr.ap[:, 0:CW], aa, aa.ap[:, 0:CW], aa, aa.ap[:, 0:CW], ALU.mult)
        yield
        act(rr, rr.ap[:, 0:CW], rr, rr.ap[:, 0:CW], AF.Ln, scale=-1.0, bias=epsb.ap[:, 1:2], extra_reads=[epsb])
        yield
        act(rr, rr.ap[:, 0:CW], rr, rr.ap[:, 0:CW], AF.Exp, scale=0.5)
        yield
        tt("dve", ig, ig.ap[:, 0:CW], ig, ig.ap[:, 0:CW], rr, rr.ap[:, 0:CW], ALU.mult)
        yield
        tt("dve", ig, ig.ap[:, 0:CW], ig, ig.ap[:, 0:CW], xcb_t, xcb_ap, ALU.mult)
        yield
        H = hbuf[ch]
        cp("pool", H, H.ap[:, 0:2], H, H.ap[:, 256:258])
        yield
        P.op("dve", lambda e, H=H, aa=aa, ig=ig: e.tensor_tensor_scan(
            out=H.ap[:, 2:258], data0=aa.ap[:, 0:CW], data1=ig.ap[:, 0:CW], initial=H.ap[:, 1:2],
            op0=ALU.mult, op1=ALU.add), reads=[aa, ig, H], writes=[H])
        yield
        Y = yown[ch]
        if m == 0:
            ts("dve", Y, Y.ap[:, :], H, H.ap[:, :], pcol("sel", 0), None, ALU.mult)
        else:
            stt(Y, Y.ap[:, :], H, H.ap[:, :], pcol("sel", m), Y, Y.ap[:, :], ALU.mult, ALU.add)
        yield

    def rnn_chain(r, m, xb):
        tmp = rc[r]
        bank = ps[6 + r]
        xa_t, xa_ap = tmp["xcb"], tmp["xcb"].ap[:, 0:CW]
        xb_t = xcb2[r]
        xb_ap = xcb2[r].ap[:, 0:CW]
        yield from rnn_front(r, xb, bank, xa_t, xa_ap)
        g1 = rnn_back(r, m, bank, tmp, xa_t, xa_ap)
        g2 = rnn_front(r + 2, xb, bank, xb_t, xb_ap)
        next(g1)
        yield
        live = [g1, g2]
        while live:
            nxt = []
            for g in live:
                try:
                    next(g)
                    nxt.append(g)
                except StopIteration:
                    pass
            live = nxt
            yield
        yield from rnn_back(r + 2, m, bank, tmp, xb_t, xb_ap)

    def a_src(c):
        return lambda k: xT[k * 128:(k + 1) * 128, c * CW:(c + 1) * CW]

    def seq(*gens):
        for g in gens:
            yield from g

    def flagged(gen, flag):
        yield from gen
        flag[0] = True

    k_done = [0]
    q_done = [0]
    bt_ready = [False]

    def q_tail(h, parb):
        k_done[0] += 1
        while not (b_ready[0] and bt_ready[0]) or k_done[0] < 4:
            yield
        if h >= 2:
            while q_done[0] < 2:
                yield
        yield from qk_chain(h, h, xbp[parb], W, dvc.ap[:, 8:9], Ct[parb], St[parb],
                            (kc[h]["ksq"], kc[h]["knb"]), None, banks=(ps[h % 2], ps[2 + h % 2]))
        q_done[0] += 1

    def phase_a_chains(c):
        par = c % 2
        xb = xbp[par]
        ch = [qk_chain(h, 4 + h, xb, CW, pcol("kw"), Ct[par], St[par], KTt[h][c], KT[:, h, c * CW:(c + 1) * CW])
              for h in range(4)]
        if c % 4 == 3:
            k_done[0] = 0
            q_done[0] = 0
            ch = [seq(ch[h], q_tail(h, (c + 1) % 2)) for h in range(4)]
        ch.append(v_chain(c, xb))
        if c % 4 == 3:
            xbo = xbp[(c + 1) % 2]
            ch.append(seq(rnn_chain(0, c % 4, xb), gr_chain(0, xbo, None)))
            ch.append(seq(rnn_chain(1, c % 4, xb), gr_chain(1, xbo, None)))
        else:
            ch.append(rnn_chain(0, c % 4, xb))
            ch.append(rnn_chain(1, c % 4, xb))
        ch.append(bg_chain(6))
        return ch

    xmid_t = [T(None, f"xmid{n}") for n in range(NSLOT)]

    b_ready = [False]

    def gr_chain(r, xb, mixl):
        while not b_ready[0]:
            yield
        tmp = rc[r]
        gl = tmp["r"]
        bank = ps[6 + r]
        for ch in (r, r + 2):
            w = acquire()
            while w is None:
                yield
                w = acquire()
            load_w(w, 16 + ch, win_t, win_s)
            yield
            for k in range(8):
                mm(bank, bank.ap[:, 0:W], w, w.ap[:, k, :], xb[k], xb[k].ap[:, 0:W], k == 0, k == 7)
            release(w)
            yield
            act(gl, gl.ap[:, :], bank, bank.ap[:, 0:W], AF.Gelu_apprx_tanh)
            yield
            tt("dve", yown[ch], yown[ch].ap[:, :], yown[ch], yown[ch].ap[:, :], gl, gl.ap[:, :], ALU.mult)
            yield

    def phase_b(n, par):
        xb = xbp[par]
        mixl = xbp[1 - par]
        C, Sn = Ct[par], St[par]
        QT = [kc[h]["ksq"] for h in range(4)]
        QTa = [kc[h]["ksq"] for h in range(4)]
        QTb = [kc[h]["knb"] for h in range(4)]
        ysq = xsq[0]
        for ch in range(4):
            act(ysq, ysq.ap[:, :], yown[ch], yown[ch].ap[:, :], AF.Square)
            mm(ps[4], ps[4].ap[:, 0:W], ones_bf, ones_bf.ap[:, :], ysq, ysq.ap[:, :], ch == 0, ch == 3)
        ry = rc[0]["a"]
        rstd_from_psum(ps[4], W, 1.0 / 512.0, ry)
        for ch in range(4):
            tt("dve", mixl[4 + ch], mixl[4 + ch].ap[:, :], yown[ch], yown[ch].ap[:, :], ry, ry.ap[:, :], ALU.mult)
        wo = []
        for dc in range(NWST - 1):
            w = acquire()
            assert w is not None
            load_w(w, dc, wout_t, wout_s)
            wo.append(w)
        nkt = 8 * n + 8
        steps = [(h, kt) for h in range(4) for kt in range(nkt)]
        pbuf = [(rc[0]["xcb"], rc[1]["xcb"]), (xsq[0], xsq[1])]

        def qk_mm(idx):
            h, kt = steps[idx]
            c = kt // 2
            ksl = slice(kt * 128, (kt + 1) * 128)
            sA, sB = ps[(idx % 2) * 2], ps[(idx % 2) * 2 + 1]
            mm(sA, sA.ap[:, 0:W], KTt[h][c], KT[:, h, ksl], QTa[h], QTa[h].ap[:, :], True, True)
            mm(sB, sB.ap[:, 0:W], KTt[h][c], KT[:, h, ksl], QTb[h], QTb[h].ap[:, :], True, True)

        qk_mm(0)
        deferred = []
        for idx, (h, kt) in enumerate(steps):
            if idx + 1 < len(steps):
                qk_mm(idx + 1)
            sA, sB = ps[(idx % 2) * 2], ps[(idx % 2) * 2 + 1]
            p0, p1 = pbuf[idx % 2]
            act(p0, p0.ap[:, :], sA, sA.ap[:, 0:W], AF.Exp)
            act(p1, p1.ap[:, :], sB, sB.ap[:, 0:W], AF.Exp)
            mi = kt - (8 * n - 1)
            if mi >= 0:
                tt("dve", p0, p0.ap[:, :], p0, p0.ap[:, :], maskb, maskb.ap[:, mi, :], ALU.mult)
                tt("dve", p1, p1.ap[:, :], p1, p1.ap[:, :], maskb, maskb.ap[:, mi, :], ALU.mult)
            O0, O1, D0, D1 = ps[4], ps[5], ps[6], ps[7]
            vt = VVt[kt]
            vsl = VV[:, kt, h * 128:(h + 1) * 128]
            st, sp_ = kt == 0, kt == nkt - 1
            mm(O0, O0.ap[:, 0:W], vt, vsl, p0, p0.ap[:, :], st, sp_)
            mm(O1, O1.ap[:, 0:W], vt, vsl, p1, p1.ap[:, :], st, sp_)
            mm(D0, D0.ap[:, 0:W], ones_bf, ones_bf.ap[:, :], p0, p0.ap[:, :], st, sp_)
            mm(D1, D1.ap[:, 0:W], ones_bf, ones_bf.ap[:, :], p1, p1.ap[:, :], st, sp_)
            if kt == nkt - 1:
                hp = h % 2
                o0s, o1s, at, rs = rc[hp]["xc"], rc[hp]["r"], rc[hp]["ig"], rc[hp]["a"]
                rd0, rd1 = kc[hp]["rk"], kc[hp]["kn"]
                cp("dve", o0s, o0s.ap[:, :], O0, O0.ap[:, 0:W])
                cp("dve", o1s, o1s.ap[:, :], O1, O1.ap[:, 0:W])
                act(rd0, rd0.ap[:, :], D0, D0.ap[:, 0:W], AF.Ln, bias=epsb.ap[:, 2:3], extra_reads=[epsb])
                act(rd1, rd1.ap[:, :], D1, D1.ap[:, 0:W], AF.Ln, bias=epsb.ap[:, 2:3], extra_reads=[epsb])

                def rest(h=h, o0s=o0s, o1s=o1s, rd0=rd0, rd1=rd1):
                    act(rd0, rd0.ap[:, :], rd0, rd0.ap[:, :], AF.Exp, scale=-1.0)
                    act(rd1, rd1.ap[:, :], rd1, rd1.ap[:, :], AF.Exp, scale=-1.0)
                    tt("dve", o0s, o0s.ap[:, :], o0s, o0s.ap[:, :], rd0, rd0.ap[:, :], ALU.mult)
                    tt("dve", o1s, o1s.ap[:, :], o1s, o1s.ap[:, :], rd1, rd1.ap[:, :], ALU.mult)
                    stt(mixl[h], mixl[h].ap[:, :], o1s, o1s.ap[:, :], dvc.ap[:, 9:10], o0s, o0s.ap[:, :], ALU.mult, ALU.add)

                deferred.append((idx + 3, rest))
            while deferred and deferred[0][0] <= idx:
                deferred.pop(0)[1]()
        while deferred:
            deferred.pop(0)[1]()
        for h in range(4):
            asq = pbuf[h % 2][h // 2]
            bk = ps[4 + h]
            rs = rc[h % 2]["a"] if h < 2 else rc[h % 2]["ig"]
            act(asq, asq.ap[:, :], mixl[h], mixl[h].ap[:, :], AF.Square)
            mm(bk, bk.ap[:, 0:W], ones_bf, ones_bf.ap[:, :], asq, asq.ap[:, :], True, True)
            rstd_from_psum(bk, W, 1.0 / 128.0, rs)
            tt("dve", mixl[h], mixl[h].ap[:, :], mixl[h], mixl[h].ap[:, :], rs, rs.ap[:, :], ALU.mult)
        for dc in range(8):
            w = wo[dc]
            po = ps[dc % 2]
            for k in range(8):
                mm(po, po.ap[:, 0:W], w, w.ap[:, k, :], mixl[k], mixl[k].ap[:, :], k == 0, k == 7)
            release(w)
            if len(wo) < 8:
                w2 = acquire()
                assert w2 is not None
                load_w(w2, len(wo), wout_t, wout_s)
                wo.append(w2)
            xm = rTa if dc % 2 else rTb
            tt("dve", xm, xm.ap[:, :], po, po.ap[:, 0:W], xs[dc], xs[dc].ap[:, 0:W], ALU.add)
            dma(xmid_t[n], xmid_s[n][:, dc * W:(dc + 1) * W], xm.ap[:, :], reads=[xm], nowaw=True)

    run_chains([stage0(a_src(0), CW, xbp[0], ps[5]), gen_tables(0, CW, Ct[0], St[0])])
    for c in range(NCH):
        chains = phase_a_chains(c)
        if c % 4 != 3 and c + 1 < NCH:
            chains.append(stage0(a_src(c + 1), CW, xbp[(c + 1) % 2], ps[5]))
            chains.append(gen_tables(c + 1, CW, Ct[(c + 1) % 2], St[(c + 1) % 2]))
        if c % 4 == 3:
            nb = c // 4
            b_ready[0] = False
            chains.append(flagged(stage0(lambda k, nb=nb: xoT[k * 128:(k + 1) * 128, nb * W:(nb + 1) * W], W,
                                         xbp[(c + 1) % 2], ps[5]), b_ready))
            bt_ready[0] = False
            chains.append(flagged(gen_tables(32 + nb, W, Ct[(c + 1) % 2], St[(c + 1) % 2]), bt_ready))
        run_chains(chains)
        if c % 4 == 3:
            phase_b(c // 4, (c + 1) % 2)
            if c + 1 < NCH:
                run_chains([stage0(a_src(c + 1), CW, xbp[(c + 1) % 2], ps[5]),
                            gen_tables(c + 1, CW, Ct[(c + 1) % 2], St[(c + 1) % 2])])
    for _ in bg:
        pass

    last = {e: (P.ins[e][-1] if P.ins[e] else None) for e in P.ENGS}

    def otile(ap, name):
        t = T(ap, name)
        t.readers = {e: i for e, i in last.items() if i is not None and e != "sp"}
        return t

    WDN = otile(arena[:, 0:24576].rearrange("p (k x) -> p k x", x=1024), "WDN")
    MT = [[otile(arena[:, 24576 + (g * 24 + ci) * 256:24576 + (g * 24 + ci + 1) * 256], f"mt{g}_{ci}")
           for ci in range(24)] for g in range(4)]
    H2 = [[otile(arena[:, 49152 + (g * 8 + k) * W:49152 + (g * 8 + k + 1) * W], f"h2{g}_{k}")
           for k in range(8)] for g in range(4)]
    out_t = T(None, "outT")
    for dc in range(8):
        dma(WDN, WDN.ap[:, :, dc * 128:(dc + 1) * 128], wdn_s[dc].rearrange("p (k x) -> p k x", x=128),
            reads=[wdn_t], nowaw=True)
    cfo, _ = PC["cfw"]
    cbo, _ = PC["cfb"]

    def ffn_chain(i, grp, cis):
        ur = [kc[2 * i]["rk"], kc[2 * i]["kn"]]
        cv = [kc[2 * i + 1]["rk"], kc[2 * i + 1]["kn"]]
        gl = rc[i]["xc"]
        cis = list(cis)
        nxt = None
        for ii, ci in enumerate(cis):
            if nxt is not None and len(nxt) == 2:
                wpair = nxt
            else:
                wpair = nxt or []
                for cc in (ci, 24 + ci)[len(wpair):]:
                    w = acquire()
                    while w is None:
                        yield
                        w = acquire()
                    load_w(w, cc, wup_t, wup_s)
                    wpair.append((cc, w))
            nxt = []
            if ii + 1 < len(cis):
                for cc in (cis[ii + 1], 24 + cis[ii + 1]):
                    w = acquire()
                    if w is None:
                        break
                    load_w(w, cc, wup_t, wup_s)
                    nxt.append((cc, w))
            yield
            for g in range(4):
                n = grp * 4 + g
                for hi, (cc, w) in enumerate(wpair):
                    pu = ps[2 * i + hi]
                    for k in range(8):
                        mm(pu, pu.ap[:, 0:W], w, w.ap[:, k, :], H2[g][k], H2[g][k].ap, k == 0, k == 7)
                    yield
                    wc = lambda tap, cc=cc: prm.ap[:, cfo + cc * 3 + tap:cfo + cc * 3 + tap + 1]
                    if n == 0:
                        act(ur[hi], ur[hi].ap[:, 0:257], pu, pu.ap[:, 0:257], AF.Copy)
                        act(cv[hi], cv[hi].ap[:, 0:CW], pu, pu.ap[:, 2:258], AF.Identity, scale=wc(2),
                            bias=prm.ap[:, cbo + cc:cbo + cc + 1])
                        ts("dve", ur[hi], ur[hi].ap[:, 0:2], ur[hi], ur[hi].ap[:, 0:2], pcol("hflag"), None, ALU.mult)
                        yield
                        stt(cv[hi], cv[hi].ap[:, 0:CW], ur[hi], ur[hi].ap[:, 1:257], wc(1), cv[hi], cv[hi].ap[:, 0:CW], ALU.mult, ALU.add)
                        yield
                        stt(cv[hi], cv[hi].ap[:, 0:CW], ur[hi], ur[hi].ap[:, 0:256], wc(0), cv[hi], cv[hi].ap[:, 0:CW], ALU.mult, ALU.add)
                        yield
                    else:
                        act(cv[hi], cv[hi].ap[:, 0:CW], pu, pu.ap[:, 2:258], AF.Identity, scale=wc(2),
                            bias=prm.ap[:, cbo + cc:cbo + cc + 1])
                        yield
                        stt(cv[hi], cv[hi].ap[:, 0:CW], pu, pu.ap[:, 1:257], wc(1), cv[hi], cv[hi].ap[:, 0:CW], ALU.mult, ALU.add)
                        yield
                        stt(cv[hi], cv[hi].ap[:, 0:CW], pu, pu.ap[:, 0:256], wc(0), cv[hi], cv[hi].ap[:, 0:CW], ALU.mult, ALU.add)
                        yield
                act(gl, gl.ap[:, 0:CW], cv[0], cv[0].ap[:, 0:CW], AF.Gelu_apprx_tanh)
                yield
                tt("pool", MT[g][ci], MT[g][ci].ap, gl, gl.ap[:, 0:CW], cv[1], cv[1].ap[:, 0:CW], ALU.mult)
                yield
            for _, w in wpair:
                release(w)

    def down_chain(i, grp):
        xm, ob = rc[i]["r"], rc[i]["ig"]
        pd = ps[6 + i]
        for g in range(4):
            n = grp * 4 + g
            for dc in range(i, 8, 2):
                dma(xm, xm.ap[:, :], xmid_s[n][:, dc * W:(dc + 1) * W], reads=[xmid_t[n]])
                for ci in range(24):
                    mm(pd, pd.ap[:, 0:CW], WDN, WDN.ap[:, ci, dc * 128:(dc + 1) * 128], MT[g][ci], MT[g][ci].ap, ci == 0, ci == 23)
                    if ci % 8 == 7:
                        yield
                tt("dve", ob, ob.ap[:, 0:CW], pd, pd.ap[:, 0:CW], xm, xm.ap[:, 2:258], ALU.add)
                yield
                dma(out_t, outT[dc * 128:(dc + 1) * 128, n * CW:(n + 1) * CW], ob.ap[:, 0:CW], reads=[ob], nowaw=True)
                yield

    for grp in range(2):
        for g in range(4):
            n = grp * 4 + g
            for k in range(8):
                dma(xs[k], xs[k].ap[:, 0:W], xmid_s[n][:, k * W:(k + 1) * W], reads=[xmid_t[n]])
            for k in range(8):
                q = xsq[k % 2]
                act(q, q.ap[:, 0:W], xs[k], xs[k].ap[:, 0:W], AF.Square)
                mm(ps[5], ps[5].ap[:, 0:W], ones_bf, ones_bf.ap[:, :], q, q.ap[:, 0:W], k == 0, k == 7)
            rstd_from_psum(ps[5], W, 1.0 / 1024.0, rstd_t)
            for k in range(8):
                tt("pool" if k % 4 == 3 else "dve", H2[g][k], H2[g][k].ap, xs[k], xs[k].ap[:, 0:W],
                   rstd_t, rstd_t.ap[:, 0:W], ALU.mult)
        run_chains([ffn_chain(0, grp, range(0, 24, 2)), ffn_chain(1, grp, range(1, 24, 2))])
        run_chains([down_chain(0, grp), down_chain(1, grp)])
    P.emit([out_t])
    return nc, es


_CACHE = {}


def _params_for(inp, j):
    p = np.zeros((128, NP), np.float32)

    def put(name, arr):
        o, w = PC[name]
        arr = np.asarray(arr, np.float32)
        assert arr.shape == (128, w), (name, arr.shape)
        p[:, o:o + w] = arr

    put("n1w", inp["norm1_w"][0].reshape(8, 128).T)
    put("qw", np.tile(inp["q_norm_w"][0], 2)[:, None])
    put("kw", np.tile(inp["k_norm_w"][0], 2)[:, None])
    lam = np.zeros((128, 4), np.float32)
    lam[:64, 0] = inp["lambda_q1"][0]
    lam[:64, 1] = inp["lambda_k1"][0]
    lam[:64, 2] = inp["lambda_q2"][0]
    lam[:64, 3] = inp["lambda_k2"][0]
    put("lam", lam)
    put("subw", inp["subln_w"][0][:, None])
    crw = inp["conv_rnn_w"][0]
    put("crw", crw.reshape(4, 4, 128).transpose(2, 1, 0).reshape(128, 16))
    put("crb", inp["conv_rnn_b"][0].reshape(4, 128).T)
    put("bga", inp["b_gate_a"][0].reshape(4, 128).T)
    put("bgx", inp["b_gate_x"][0].reshape(4, 128).T)
    put("lru", inp["lru_lambda"][0].reshape(4, 128).T)
    put("rnw", inp["rnn_norm_w"][0].reshape(4, 128).T)
    put("n2w", inp["norm2_w"][0].reshape(8, 128).T)
    cfw = inp["conv_ffn_w"][0]
    put("cfw", cfw.reshape(3, 48, 128).transpose(2, 1, 0).reshape(128, 144))
    put("cfb", inp["conv_ffn_b"][0].reshape(48, 128).T)
    sel = np.zeros((128, 4), np.float32)
    sel[:, j] = 1.0
    put("sel", sel)
    put("hflag", np.full((128, 1), 0.0 if j == 0 else 1.0, np.float32))
    put("qpos", np.tile((256.0 * (4 * np.arange(8) + j) - 2.0)[None, :], (128, 1)))
    put("kpos", np.tile((256.0 * np.arange(32))[None, :], (128, 1)))
    ft = np.zeros((128, 1), np.float32)
    for pp in range(128):
        i = pp % 64
        if i < 16:
            ft[pp, 0] = (500000.0 ** (-(2.0 * (i % 8)) / 16.0)) / (2.0 * np.pi)
    put("fturn", ft)
    return p


def _consts():
    rot = np.zeros((128, 128), np.float32)
    for blk in (0, 64):
        for i in range(8):
            rot[blk + i + 8, blk + i] = -1.0
            rot[blk + i, blk + i + 8] = 1.0
    return np.concatenate([rot, np.eye(128, dtype=np.float32)], axis=1)


def _mask(j):
    m = np.zeros((128, 9, W), np.float32)
    r = np.arange(128)[:, None]
    col = np.arange(W)[None, :]
    qrel = 256 * j - 2 + col
    for mi in range(9):
        krel = 128 * (mi - 1) + r
        m[:, mi, :] = (krel <= qrel).astype(np.float32)
    return m.reshape(128, 9 * W)


def kernel(**inp):
    x = np.asarray(inp["x"], np.float32)
    if "nc" not in _CACHE:
        _CACHE["nc"] = build()
    nc, _es = _CACHE["nc"]
    wg = np.zeros((128, 8, 128), np.float32)
    for gi, key in enumerate(("w_gate_a", "w_gate_x")):
        wgt = np.asarray(inp[key][0], np.float32)
        for ch in range(4):
            for hb in range(2):
                wg[hb * 64:(hb + 1) * 64, gi * 4 + ch, hb * 64:(hb + 1) * 64] = wgt[ch * 2 + hb]
    wg = wg.reshape(128, 1024)
    cst = _consts()
    in_maps = []
    for core in range(8):
        b, j = divmod(core, 4)
        xTb = np.ascontiguousarray(x[b].T)
        xo = np.zeros((D, NSLOT, W), np.float32)
        for n in range(NSLOT):
            s0 = 256 * (4 * n + j)
            if s0 >= 2:
                xo[:, n, :] = xTb[:, s0 - 2:s0 + 256]
            else:
                xo[:, n, 2:] = xTb[:, 0:256]
        in_maps.append({
            "xT": xTb, "xoT": xo.reshape(D, NSLOT * W),
            "w_in": np.ascontiguousarray(inp["w_in"][0], np.float32),
            "w_out": np.ascontiguousarray(inp["w_out"][0], np.float32),
            "w_up": np.ascontiguousarray(inp["w_up"][0], np.float32),
            "w_down": np.ascontiguousarray(inp["w_down"][0], np.float32),
            "params": _params_for(inp, j), "wg": wg, "cst": cst, "mask": _mask(j),
        })
    res = run_bass_kernel_spmd(nc, in_maps, core_ids=list(range(8)))
    out = np.zeros((2, S, D), np.float32)
    for core in range(8):
        b, j = divmod(core, 4)
        oT = np.asarray(res.results[core]["outT"]).reshape(D, NSLOT, CW)
        for n in range(NSLOT):
            s0 = 256 * (4 * n + j)
            out[b, s0:s0 + 256, :] = oT[:, n, :].T
    return out
```

```python
import math
from contextlib import ExitStack
import numpy as np
import concourse.bass as bass
import concourse.mybir as mybir
from concourse.bass_utils import run_bass_kernel_spmd

F32 = mybir.dt.float32
BF16 = mybir.dt.bfloat16
I32 = mybir.dt.int32
AF = mybir.ActivationFunctionType
ALU = mybir.AluOpType

D = 1024
S = 8192
NCH = 32
CW = 256
W = 258
NSLOT = 8
EPS = 1e-6
LAMBDA_INIT = 0.8 - 0.6 * math.exp(0.0)
SAME_SYNC = True
RAW_ONLY_SAME = True
EMBED_WAIT = True
N_A_TILES = 32

PC = {}
_o = 0
for _n, _w in [("n1w", 8), ("qw", 1), ("kw", 1), ("lam", 4), ("subw", 1), ("crw", 16), ("crb", 4),
               ("bga", 4), ("bgx", 4), ("lru", 4), ("rnw", 4), ("n2w", 8), ("cfw", 144), ("cfb", 48),
               ("sel", 4), ("hflag", 1), ("qpos", 8), ("kpos", 32), ("fturn", 1)]:
    PC[_n] = (_o, _w)
    _o += _w
NP = _o


class T:
    __slots__ = ("ap", "name", "last_w", "readers", "sem", "cnt")

    def __init__(self, ap, name):
        self.ap = ap
        self.name = name
        self.last_w = None
        self.readers = {}
        self.sem = None
        self.cnt = 0


class Ins:
    __slots__ = ("eng", "fn", "deps", "is_dma", "sem", "val", "signal", "sigval")

    def __init__(self, eng, fn):
        self.eng = eng
        self.fn = fn
        self.deps = []
        self.is_dma = False
        self.sem = None
        self.val = 0
        self.signal = False
        self.sigval = 0


class Prog:
    ENGS = ("pe", "act", "dve", "pool", "sp")

    def __init__(self, nc, es):
        self.nc = nc
        self.es = es
        self.ins = {e: [] for e in self.ENGS}
        self.nsem = 0

    def new_sem(self, name):
        self.nsem += 1
        return self.es.enter_context(self.nc.semaphore(name))

    def op(self, eng, fn, reads=(), writes=(), dma=False, nowaw=False):
        ins = Ins(eng, fn)
        ins.is_dma = dma
        deps = []
        raw = set()
        for t in reads:
            if t.last_w is not None:
                deps.append(t.last_w)
                raw.add(id(t.last_w))
        for t in writes:
            if t.last_w is not None and not (nowaw and t.last_w.is_dma and t.last_w.eng == eng):
                deps.append(t.last_w)
            deps.extend(t.readers.values())
        seen = set()
        for d in deps:
            if id(d) in seen or d is ins:
                continue
            seen.add(id(d))
            if not d.is_dma and d.eng == eng and (eng == "pe" or not SAME_SYNC):
                continue
            if RAW_ONLY_SAME and not d.is_dma and d.eng == eng and id(d) not in raw:
                continue
            ins.deps.append(d)
            d.signal = True
        if dma:
            t = writes[0]
            if t.sem is None:
                t.sem = self.new_sem("d_" + t.name)
            t.cnt += 1
            ins.sem = t.sem
            ins.val = 16 * t.cnt
        for t in reads:
            t.readers[eng if not dma else ("dma", id(ins))] = ins
        for t in writes:
            t.last_w = ins
            t.readers = {}
        self.ins[eng].append(ins)
        return ins

    def emit(self, final_waits):
        nc = self.nc
        esem = {e: self.es.enter_context(nc.semaphore("e_" + e)) for e in self.ENGS}
        for e in self.ENGS:
            c = 0
            for i in self.ins[e]:
                if i.signal and not i.is_dma:
                    c += 1
                    i.sigval = c
        block = self.es.enter_context(nc.Block())

        def run(e, eng):
            seen = {}
            for i in self.ins[e]:
                need = {}
                for d in i.deps:
                    if d.is_dma:
                        key, v, sem = ("dma", id(d.sem)), d.val, d.sem
                    else:
                        key, v, sem = d.eng, d.sigval, esem[d.eng]
                    if seen.get(key, 0) >= v:
                        continue
                    seen[key] = v
                    need[key] = (sem, v)
                waits = list(need.values())
                emb = None
                if waits and EMBED_WAIT:
                    emb = waits.pop()
                for sem, v in waits:
                    eng.wait_ge(sem, v)
                bi = i.fn(eng)
                if emb is not None:
                    bi = bi._wait_ge(emb[0], emb[1])
                if i.is_dma:
                    bi.then_inc(i.sem, 16)
                elif i.signal:
                    bi.then_inc(esem[e], 1)
            if e == "sp":
                for t in final_waits:
                    eng.wait_ge(t.sem, 16 * t.cnt)

        @block.tensor
        def _(eng):
            run("pe", eng)

        @block.scalar
        def _(eng):
            run("act", eng)

        @block.vector
        def _(eng):
            run("dve", eng)

        @block.gpsimd
        def _(eng):
            run("pool", eng)

        @block.sync
        def _(eng):
            run("sp", eng)


def build():
    nc = bass.Bass("TRN2", target_bir_lowering=False, dynamic_dma_scratch_size=512)
    es = ExitStack()
    P = Prog(nc, es)

    def din(name, shape, dt=F32):
        return nc.dram_tensor(name, shape, dt, kind="ExternalInput").ap()

    xT = din("xT", [D, S])
    xoT = din("xoT", [D, NSLOT * W])
    w_in = din("w_in", [D, 2560])
    w_out = din("w_out", [D, D])
    w_up = din("w_up", [D, 6144])
    w_down = din("w_down", [3072, D])
    params_d = din("params", [128, NP])
    wg_d = din("wg", [128, 8 * 128])
    cst_d = din("cst", [128, 2 * 128])
    mask_d = din("mask", [128, 9 * W])
    outT = nc.dram_tensor("outT", [D, NSLOT * CW], F32, kind="ExternalOutput").ap()
    win_s = nc.dram_tensor("win_s", [20, 128, 8 * 128], BF16).ap()
    wout_s = nc.dram_tensor("wout_s", [8, 128, 8 * 128], BF16).ap()
    wup_s = nc.dram_tensor("wup_s", [48, 128, 8 * 128], BF16).ap()
    wdn_s = nc.dram_tensor("wdn_s", [8, 128, 24 * 128], BF16).ap()
    xmid_s = nc.dram_tensor("xmid_s", [NSLOT, 128, 8 * W], F32).ap()

    def sb(name, shape, dt):
        return nc.alloc_sbuf_tensor(name, shape, dt)

    def tile(name, shape, dt):
        h = sb(name, shape, dt)
        return T(h, name)

    arena = sb("arena", [128, 65536], BF16)
    KT = arena[:, 0:32768].rearrange("p (h s) -> p h s", h=4)
    KTt = [[T(KT, f"KT{h}_{c}") for c in range(NCH)] for h in range(4)]
    VV = arena[:, 32768:65536].rearrange("p (b x) -> p b x", x=512)
    VVt = [T(VV, f"V{b}") for b in range(64)]
    prm = tile("prm", [128, NP], F32)
    wgb = tile("wgb", [128, 8, 128], BF16)
    cstb = tile("cstb", [128, 2, 128], BF16)
    maskb = tile("maskb", [128, 9, W], BF16)
    ones_bf = tile("ones_bf", [128, 128], BF16)
    bones_bf = tile("bones_bf", [128, 128], BF16)
    ones_f = tile("ones_f", [128, 128], F32)
    dC = tile("dC", [128, W], F32)
    dS = tile("dS", [128, W], F32)
    c0s0 = tile("c0s0", [128, 80], F32)
    dvc = tile("dvc", [128, 16], F32)
    hbuf = [tile(f"hbuf{c}", [128, W], F32) for c in range(4)]
    xr = [tile(f"xr{c}", [128, 260], BF16) for c in range(4)]
    yown = [tile(f"yown{c}", [128, W], F32) for c in range(4)]

    xs = [tile(f"xs{k}", [128, W], F32) for k in range(8)]
    xsq = [tile(f"xsq{i}", [128, W], BF16) for i in range(2)]
    xbp = [[tile(f"xb{p}_{k}", [128, W], BF16) for k in range(8)] for p in range(2)]
    Ct = [tile(f"Ct{p}", [128, W], F32) for p in range(2)]
    St = [tile(f"St{p}", [128, W], F32) for p in range(2)]
    rTa = tile("rTa", [128, W], F32)
    rTb = tile("rTb", [128, W], F32)
    rstd_t = tile("rstd", [128, W], F32)
    kc = [dict(rk=tile(f"k_rk{h}", [128, W], F32), kn=tile(f"k_kn{h}", [128, W], F32),
               ksq=tile(f"k_ksq{h}", [128, W], BF16), knb=tile(f"k_knb{h}", [128, W], BF16)) for h in range(4)]
    rc = [dict(xc=tile(f"r_xc{r}", [128, W], F32), r=tile(f"r_r{r}", [128, W], F32),
               ig=tile(f"r_ig{r}", [128, W], F32), a=tile(f"r_a{r}", [128, W], F32),
               xcb=tile(f"r_xcb{r}", [128, W], BF16)) for r in range(2)]
    NWST = 7
    wst = [tile(f"wst{i}", [128, 8, 128], BF16) for i in range(NWST)]
    dgb = tile("dgb", [128, 16, 128], BF16)
    ti32 = T(rTa.ap[:, :].bitcast(I32), "ti32")
    ft = {"kf": kc[0]["rk"], "rk": kc[1]["rk"], "kn": kc[0]["kn"], "kt1": kc[1]["kn"], "kt2": kc[2]["kn"],
          "rS": rTb}

    def tf(name):
        m = {"sc_r": "kf", "sc_nf": "rk", "sc_y": "kn", "su0": "kt1", "su1": "kt2", "iota": "rS"}
        return ft[m.get(name, name)]

    ps = [T(es.enter_context(nc.psum_tensor(f"ps{i}", [128, 512], F32)), f"ps{i}") for i in range(8)]

    def pcol(name, i=0, n=1):
        o, w = PC[name]
        return prm.ap[:, o + i:o + i + n]

    wst_i = [0]

    def next_wst():
        t = wst[wst_i[0] % NWST]
        wst_i[0] += 1
        return t

    def dma(out_t, out_ap, in_ap, reads=(), nowaw=False, q="sp"):
        P.op(q, lambda e: e.dma_start(out=out_ap, in_=in_ap), reads=reads, writes=[out_t], dma=True, nowaw=nowaw)

    def act(out_t, out_ap, in_t, in_ap, func, bias=None, scale=None, extra_reads=()):
        kw = {}
        if bias is not None:
            kw["bias"] = bias
        if scale is not None:
            kw["scale"] = scale
        P.op("act", lambda e: e.activation(out=out_ap, in_=in_ap, func=func, **kw),
             reads=[in_t, prm, dvc] + list(extra_reads), writes=[out_t])

    def mm(out_t, out_ap, lt, l_ap, rt, r_ap, start, stop):
        P.op("pe", lambda e: e.matmul(out_ap, l_ap, r_ap, start=start, stop=stop),
             reads=[lt, rt], writes=[out_t])

    def tt(eng, out_t, out_ap, a_t, a_ap, b_t, b_ap, op):
        P.op(eng, lambda e: e.tensor_tensor(out=out_ap, in0=a_ap, in1=b_ap, op=op),
             reads=[a_t, b_t], writes=[out_t])

    def ts(eng, out_t, out_ap, a_t, a_ap, s1, s2, op0, op1=None, extra_reads=()):
        if op1 is None:
            P.op(eng, lambda e: e.tensor_scalar(out=out_ap, in0=a_ap, scalar1=s1, scalar2=None, op0=op0),
                 reads=[a_t, prm, dvc] + list(extra_reads), writes=[out_t])
        else:
            P.op(eng, lambda e: e.tensor_scalar(out=out_ap, in0=a_ap, scalar1=s1, scalar2=s2, op0=op0, op1=op1),
                 reads=[a_t, prm, dvc] + list(extra_reads), writes=[out_t])

    def stt(out_t, out_ap, a_t, a_ap, scalar, b_t, b_ap, op0, op1, extra_reads=()):
        P.op("dve", lambda e: e.scalar_tensor_tensor(out=out_ap, in0=a_ap, scalar=scalar, in1=b_ap, op0=op0, op1=op1),
             reads=[a_t, b_t, prm, dvc] + list(extra_reads), writes=[out_t])

    def cp(eng, out_t, out_ap, in_t, in_ap):
        P.op(eng, lambda e: e.tensor_copy(out=out_ap, in_=in_ap), reads=[in_t], writes=[out_t])

    def memset(eng, t, ap, v):
        P.op(eng, lambda e: e.memset(ap, v), writes=[t])

    def rstd_from_psum(pt, n, inv_count, out_t):
        act(out_t, out_t.ap[:, 0:n], pt, pt.ap[:, 0:n], AF.Ln, bias=epsb.ap[:, 0:1], scale=inv_count, extra_reads=[epsb])
        act(out_t, out_t.ap[:, 0:n], out_t, out_t.ap[:, 0:n], AF.Exp, scale=-0.5)

    NSTG = 2
    wstage_f = [tile(f"wsf{i}", [128, 512], F32) for i in range(NSTG)]
    wstage_b = [tile(f"wsb{i}", [128, 512], BF16) for i in range(NSTG)]
    cstf = wstage_f[0]
    iota_f = tf("iota")
    yg = yown
    epsb = tile("epsb", [128, 3], F32)
    memset("pool", epsb, epsb.ap[:, 0:1], EPS)
    memset("pool", epsb, epsb.ap[:, 1:2], 1.0)
    memset("pool", epsb, epsb.ap[:, 2:3], 1e-30)
    dma(prm, prm.ap[:, :], params_d)
    dma(cstf, cstf.ap[:, 0:256], cst_d)
    memset("pool", ones_bf, ones_bf.ap[:, :], 1.0)
    memset("pool", ones_f, ones_f.ap[:, :], 1.0)
    memset("pool", bones_bf, bones_bf.ap[:, :], 0.0)
    memset("pool", bones_bf, bones_bf.ap[0:64, 0:64], 1.0)
    memset("pool", bones_bf, bones_bf.ap[64:128, 64:128], 1.0)
    for c in range(4):
        memset("pool", hbuf[c], hbuf[c].ap[:, :], 0.0)
        memset("pool", xr[c], xr[c].ap[:, :], 0.0)
        memset("pool", yown[c], yown[c].ap[:, :], 0.0)
    P.op("pool", lambda e: e.iota(ti32.ap, pattern=[[1, W]], base=0, channel_multiplier=0), writes=[ti32])
    cp("dve", iota_f, iota_f.ap[:, :], ti32, ti32.ap)
    cp("dve", cstb, cstb.ap[:, 0, :], cstf, cstf.ap[:, 0:128])
    cp("dve", cstb, cstb.ap[:, 1, :], cstf, cstf.ap[:, 128:256])
    for i2 in range(2):
        wgs = wstage_f[1]
        dma(wgs, wgs.ap[:, :], wg_d[:, i2 * 512:(i2 + 1) * 512])
        for i in range(4):
            cp("pool", wgb, wgb.ap[:, i2 * 4 + i, :], wgs, wgs.ap[:, i * 128:(i + 1) * 128])
    for i in range(9):
        st = xs[i % 8]
        dma(st, st.ap[:, :], mask_d[:, i * W:(i + 1) * W])
        cp("pool", maskb, maskb.ap[:, i, :], st, st.ap[:, :])

    def sincos_turns(y_t, n, out_sin_t, out_sin_ap, out_cos_t, out_cos_ap):
        r = tf("sc_r")
        nf = tf("sc_nf")
        for (off, o_t, o_ap) in ((0.0, out_sin_t, out_sin_ap), (0.25, out_cos_t, out_cos_ap)):
            ts("dve", r, r.ap[:, 0:n], y_t, y_t.ap[:, 0:n], off, None, ALU.add)
            cp("dve", ti32, ti32.ap[:, 0:n], r, r.ap[:, 0:n])
            cp("dve", nf, nf.ap[:, 0:n], ti32, ti32.ap[:, 0:n])
            tt("dve", r, r.ap[:, 0:n], r, r.ap[:, 0:n], nf, nf.ap[:, 0:n], ALU.subtract)
            act(o_t, o_ap, r, r.ap[:, 0:n], AF.Sin, scale=6.283185)

    yv = tf("sc_y")
    ts("dve", yv, yv.ap[:, :], iota_f, iota_f.ap[:, :], pcol("fturn"), None, ALU.mult)
    sincos_turns(yv, W, dS, dS.ap[:, :], dC, dC.ap[:, :])
    ts("dve", yv, yv.ap[:, 0:32], prm, pcol("kpos", 0, 32), pcol("fturn"), None, ALU.mult)
    ts("dve", yv, yv.ap[:, 32:40], prm, pcol("qpos", 0, 8), pcol("fturn"), None, ALU.mult)
    sincos_turns(yv, 40, c0s0, c0s0.ap[:, 40:80], c0s0, c0s0.ap[:, 0:40])
    t0 = tf("su0")
    act(t0, t0.ap[:, 0:4], prm, pcol("lru", 0, 4), AF.Exp, scale=-1.0)
    act(t0, t0.ap[:, 0:4], t0, t0.ap[:, 0:4], AF.Ln, bias=epsb.ap[:, 1:2], extra_reads=[epsb])
    ts("dve", dvc, dvc.ap[:, 0:4], t0, t0.ap[:, 0:4], -8.0, None, ALU.mult)
    ts("dve", dvc, dvc.ap[:, 4:8], t0, t0.ap[:, 0:4], -16.0, None, ALU.mult)
    ts("dve", dvc, dvc.ap[:, 8:9], prm, pcol("qw"), 0.125, None, ALU.mult)
    ts("dve", dvc, dvc.ap[:, 10:11], prm, pcol("subw"), 1.0 - LAMBDA_INIT, None, ALU.mult)
    lo, _ = PC["lam"]
    t1 = tf("su1")
    tt("dve", t1, t1.ap[0:64, 0:1], prm, prm.ap[0:64, lo:lo + 1], prm, prm.ap[0:64, lo + 1:lo + 2], ALU.mult)
    tt("dve", t1, t1.ap[0:64, 1:2], prm, prm.ap[0:64, lo + 2:lo + 3], prm, prm.ap[0:64, lo + 3:lo + 4], ALU.mult)
    mm(ps[0], ps[0].ap[:, 0:2], ones_f, ones_f.ap[0:64, :], t1, t1.ap[0:64, 0:2], True, True)
    act(t0, t0.ap[:, 0:2], ps[0], ps[0].ap[:, 0:2], AF.Exp)
    tt("dve", t0, t0.ap[:, 2:3], t0, t0.ap[:, 1:2], t0, t0.ap[:, 0:1], ALU.subtract)
    ts("dve", dvc, dvc.ap[:, 9:10], t0, t0.ap[:, 2:3], -LAMBDA_INIT, None, ALU.add)

    dvb = tile("dvb", [128, 8], F32)
    ts("dve", dvb, dvb.ap[:, 0:4], prm, pcol("bga", 0, 4), -1.0, None, ALU.mult)
    ts("dve", dvb, dvb.ap[:, 4:8], prm, pcol("bgx", 0, 4), -1.0, None, ALU.mult)
    for i in range(16):
        o_, _w = PC["crw"]
        ts("dve", dgb, dgb.ap[:, i, :], cstf, cstf.ap[:, 128:256], prm.ap[:, o_ + i:o_ + i + 1], None, ALU.mult)

    stg_i = [0]
    win_t = T(None, "win_s")
    wout_t = T(None, "wout_s")
    wup_t = T(None, "wup_s")
    wdn_t = T(None, "wdn_s")

    def prep(src, rows_k, ncols, scale_fn, dst_t, dst, kdim, engs=("pool",)):
        pieces = [(k, c0, min(512, ncols - c0)) for k in range(rows_k) for c0 in range(0, ncols, 512)]
        npc = len(pieces)
        base = stg_i[0]
        stg_i[0] += npc

        def bufs(j):
            i = (base + j) % NSTG
            return wstage_f[i], wstage_b[i]

        for s_ in range(npc + 2):
            if s_ < npc:
                k, c0, cw = pieces[s_]
                sf, sbb = bufs(s_)
                dma(sf, sf.ap[:, 0:cw], src[k * 128:(k + 1) * 128, c0:c0 + cw])
            j = s_ - 1
            if 0 <= j < npc:
                k, c0, cw = pieces[j]
                sf, sbb = bufs(j)
                sc = scale_fn(k)
                eng_ = engs[j % len(engs)]
                if sc is None:
                    cp(eng_, sbb, sbb.ap[:, 0:cw], sf, sf.ap[:, 0:cw])
                elif eng_ == "act":
                    act(sbb, sbb.ap[:, 0:cw], sf, sf.ap[:, 0:cw], AF.Copy, scale=sc)
                else:
                    ts(eng_, sbb, sbb.ap[:, 0:cw], sf, sf.ap[:, 0:cw], sc, 1.0, ALU.mult, ALU.mult)
            j = s_ - 2
            if 0 <= j < npc:
                k, c0, cw = pieces[j]
                sf, sbb = bufs(j)
                nchk = cw // 128
                dst_ap = dst[c0 // 128:c0 // 128 + nchk, :, k * 128:(k + 1) * 128].rearrange("c p x -> p c x")
                dma(dst_t, dst_ap, sbb.ap[:, 0:cw].rearrange("p (c x) -> p c x", x=128), reads=[sbb], nowaw=True)
            yield

    for _ in prep(w_in, 8, 2560, lambda k: pcol("n1w", k), win_t, win_s, 8, engs=("pool", "dve", "act")):
        pass

    def wout_scale(k):
        return dvc.ap[:, 10:11] if k < 4 else pcol("rnw", k - 4)

    def run_chains(gens):
        live = list(gens)
        while live:
            nxt = []
            for g in live:
                try:
                    next(g)
                    nxt.append(g)
                except StopIteration:
                    pass
            live = nxt

    def wout_scale(k):
        return dvc.ap[:, 10:11] if k < 4 else pcol("rnw", k - 4)

    def prep_all_bg():
        yield from prep(w_out, 8, 1024, wout_scale, wout_t, wout_s, 8)
        yield from prep(w_up, 8, 6144, lambda k: pcol("n2w", k), wup_t, wup_s, 8)
        yield from prep(w_down, 24, 1024, lambda k: None, wdn_t, wdn_s, 24)

    bg = prep_all_bg()

    def bg_chain(npieces):
        for _ in range(npieces):
            try:
                next(bg)
            except StopIteration:
                return
            yield

    wfree = list(wst)

    def acquire():
        return wfree.pop(0) if wfree else None

    def release(w):
        wfree.append(w)

    def load_w(dst_t, chunk, src_t, src):
        dma(dst_t, dst_t.ap[:, :, :], src[chunk].rearrange("p (k x) -> p k x", x=128), reads=[src_t])

    def stage0(src_cols, n, xb, stat_bank):
        for k in range(8):
            dma(xs[k], xs[k].ap[:, 0:n], src_cols(k), q="sp")
        yield
        for k in range(8):
            q = xsq[k % 2]
            act(q, q.ap[:, 0:n], xs[k], xs[k].ap[:, 0:n], AF.Square)
            mm(stat_bank, stat_bank.ap[:, 0:n], ones_bf, ones_bf.ap[:, :], q, q.ap[:, 0:n], k == 0, k == 7)
            if k % 2:
                yield
        rstd_from_psum(stat_bank, n, 1.0 / 1024.0, rstd_t)
        yield
        for k in range(8):
            tt("pool" if k % 4 == 3 else "dve", xb[k], xb[k].ap[:, 0:n], xs[k], xs[k].ap[:, 0:n],
               rstd_t, rstd_t.ap[:, 0:n], ALU.mult)
            if k % 2:
                yield

    def gen_tables(col, n, C, Sn):
        c0 = c0s0.ap[:, col:col + 1]
        s0 = c0s0.ap[:, 40 + col:41 + col]
        ts("pool", rTa, rTa.ap[:, 0:n], dC, dC.ap[:, 0:n], c0, 1.0, ALU.mult, ALU.mult, extra_reads=[c0s0])
        ts("pool", rTb, rTb.ap[:, 0:n], dS, dS.ap[:, 0:n], s0, 1.0, ALU.mult, ALU.mult, extra_reads=[c0s0])
        yield
        tt("pool", C, C.ap[:, 0:n], rTa, rTa.ap[:, 0:n], rTb, rTb.ap[:, 0:n], ALU.subtract)
        yield
        ts("pool", rTa, rTa.ap[:, 0:n], dC, dC.ap[:, 0:n], s0, 1.0, ALU.mult, ALU.mult, extra_reads=[c0s0])
        ts("pool", rTb, rTb.ap[:, 0:n], dS, dS.ap[:, 0:n], c0, 1.0, ALU.mult, ALU.mult, extra_reads=[c0s0])
        yield
        tt("pool", Sn, Sn.ap[:, 0:n], rTa, rTa.ap[:, 0:n], rTb, rTb.ap[:, 0:n], ALU.add)

    def qk_chain(h, wchunk, xb, n, wcol, C, Sn, out_t, out_ap, banks=None):
        tmp = kc[h]
        rk, kn, ksq, knb = tmp["rk"], tmp["kn"], tmp["ksq"], tmp["knb"]
        bank = ps[h] if banks is None else banks[0]
        w = acquire()
        while w is None:
            yield
            w = acquire()
        load_w(w, wchunk, win_t, win_s)
        yield
        for k in range(8):
            mm(bank, bank.ap[:, 0:n], w, w.ap[:, k, :], xb[k], xb[k].ap[:, 0:n], k == 0, k == 7)
        release(w)
        yield
        act(ksq, ksq.ap[:, 0:n], bank, bank.ap[:, 0:n], AF.Square)
        yield
        sb2 = bank.ap[:, 256:256 + n] if n <= 256 else None
        if sb2 is None:
            side = ps[4 + h] if banks is None else banks[1]
            ssq_ap = side.ap[:, 0:n]
        else:
            side = bank
            ssq_ap = sb2
        mm(side, ssq_ap, bones_bf, bones_bf.ap[:, :], ksq, ksq.ap[:, 0:n], True, True)
        yield
        act(rk, rk.ap[:, 0:n], side, ssq_ap, AF.Ln, bias=epsb.ap[:, 0:1], scale=1.0 / 64.0, extra_reads=[epsb])
        yield
        act(rk, rk.ap[:, 0:n], rk, rk.ap[:, 0:n], AF.Exp, scale=-0.5)
        yield
        stt(knb, knb.ap[:, 0:n], bank, bank.ap[:, 0:n], wcol, rk, rk.ap[:, 0:n], ALU.mult, ALU.mult)
        yield
        mm(side, ssq_ap, cstb, cstb.ap[:, 0, :], knb, knb.ap[:, 0:n], True, True)
        yield
        tt("dve", kn, kn.ap[:, 0:n], knb, knb.ap[:, 0:n], C, C.ap[:, 0:n], ALU.mult)
        yield
        tt("dve", rk, rk.ap[:, 0:n], side, ssq_ap, Sn, Sn.ap[:, 0:n], ALU.mult)
        yield
        if isinstance(out_t, tuple):
            qa, qb = out_t
            memset("pool", qa, qa.ap[64:128, 0:n], 0.0)
            memset("pool", qb, qb.ap[0:64, 0:n], 0.0)
            tt("dve", qa, qa.ap[0:64, 0:n], kn, kn.ap[0:64, 0:n], rk, rk.ap[0:64, 0:n], ALU.add)
            tt("dve", qb, qb.ap[64:128, 0:n], kn, kn.ap[64:128, 0:n], rk, rk.ap[64:128, 0:n], ALU.add)
        else:
            tt("dve", out_t, out_ap, kn, kn.ap[:, 0:n], rk, rk.ap[:, 0:n], ALU.add)

    def v_chain(c, xb):
        pv = ps[4]
        for hp in range(2):
            for hh in range(2):
                h = hp * 2 + hh
                w = acquire()
                while w is None:
                    yield
                    w = acquire()
                load_w(w, 8 + h, win_t, win_s)
                yield
                for tbk in range(2):
                    blk = tbk * 2 + hh
                    for k in range(8):
                        mm(pv, pv.ap[:, blk * 128:(blk + 1) * 128], xb[k], xb[k].ap[:, tbk * 128:(tbk + 1) * 128],
                           w, w.ap[:, k, :], k == 0, k == 7)
                    yield
                release(w)
            for tbk in range(2):
                vt = VVt[c * 2 + tbk]
                cp("dve", vt, VV[:, c * 2 + tbk, hp * 256:(hp + 1) * 256], pv, pv.ap[:, tbk * 256:(tbk + 1) * 256])
            yield

    xcb2 = [tile(f"xcb2_{r}", [128, W], BF16) for r in range(2)]

    def rnn_front(ch, xb, bank, xcb_t, xcb_ap):
        w = acquire()
        while w is None:
            yield
            w = acquire()
        load_w(w, 12 + ch, win_t, win_s)
        yield
        for k in range(8):
            mm(bank, bank.ap[:, 0:CW], w, w.ap[:, k, :], xb[k], xb[k].ap[:, 0:CW], k == 0, k == 7)
        release(w)
        yield
        X = xr[ch]
        cp("dve", X, X.ap[:, 3:259], bank, bank.ap[:, 0:CW])
        yield
        for tap in range(4):
            mm(bank, bank.ap[:, 256:512], dgb, dgb.ap[:, ch * 4 + tap, :], X, X.ap[:, tap:tap + CW], tap == 0, tap == 3)
        yield
        ts("dve", xcb_t, xcb_ap, bank, bank.ap[:, 256:512], pcol("crb", ch), None, ALU.add)
        yield
        cp("pool", X, X.ap[:, 0:3], X, X.ap[:, 256:259])
        mm(bank, bank.ap[:, 0:CW], wgb, wgb.ap[:, ch, :], xcb_t, xcb_ap, True, True)
        mm(bank, bank.ap[:, 256:512], wgb, wgb.ap[:, 4 + ch, :], xcb_t, xcb_ap, True, True)
        yield

    def rnn_back(ch, m, bank, tmp, xcb_t, xcb_ap):
        rr, ig, aa = tmp["r"], tmp["ig"], tmp["a"]
        act(rr, rr.ap[:, 0:CW], bank, bank.ap[:, 0:CW], AF.Exp, scale=-1.0, bias=dvb.ap[:, ch:ch + 1], extra_reads=[dvb])
        act(ig, ig.ap[:, 0:CW], bank, bank.ap[:, 256:512], AF.Exp, scale=-1.0, bias=dvb.ap[:, 4 + ch:5 + ch], extra_reads=[dvb])
        yield
        act(rr, rr.ap[:, 0:CW], rr, rr.ap[:, 0:CW], AF.Ln, bias=epsb.ap[:, 1:2], extra_reads=[epsb])
        yield
        act(rr, rr.ap[:, 0:CW], rr, rr.ap[:, 0:CW], AF.Exp, scale=-1.0)
        yield
        act(ig, ig.ap[:, 0:CW], ig, ig.ap[:, 0:CW], AF.Ln, bias=epsb.ap[:, 1:2], extra_reads=[epsb])
        yield
        act(ig, ig.ap[:, 0:CW], ig, ig.ap[:, 0:CW], AF.Exp, scale=-1.0)
        yield
        act(aa, aa.ap[:, 0:CW], rr, rr.ap[:, 0:CW], AF.Exp, scale=dvc.ap[:, ch:ch + 1])
        yield
        tt("dve", rr, rr.ap[:, 0:CW], aa, aa.ap[:, 0:CW], aa, aa.ap[:, 0:CW], ALU.mult)
        yield
        act(rr, rr.ap[:, 0:CW], rr, rr.ap[:, 0:CW], AF.Ln, scale=-1.0, bias=epsb.ap[:, 1:2], extra_reads=[epsb])
        yield
        act(rr, rr.ap[:, 0:CW], rr, rr.ap[:, 0:CW], AF.Exp, scale=0.5)
        yield
        tt("dve", ig, ig.ap[:, 0:CW], ig, ig.ap[:, 0:CW], rr, rr.ap[:, 0:CW], ALU.mult)
        yield
        tt("dve", ig, ig.ap[:, 0:CW], ig, ig.ap[:, 0:CW], xcb_t, xcb_ap, ALU.mult)
        yield
        H = hbuf[ch]
        cp("pool", H, H.ap[:, 0:2], H, H.ap[:, 256:258])
        yield
        P.op("dve", lambda e, H=H, aa=aa, ig=ig: e.tensor_tensor_scan(
            out=H.ap[:, 2:258], data0=aa.ap[:, 0:CW], data1=ig.ap[:, 0:CW], initial=H.ap[:, 1:2],
            op0=ALU.mult, op1=ALU.add), reads=[aa, ig, H], writes=[H])
        yield
        Y = yown[ch]
        if m == 0:
            ts("dve", Y, Y.ap[:, :], H, H.ap[:, :], pcol("sel", 0), None, ALU.mult)
        else:
            stt(Y, Y.ap[:, :], H, H.ap[:, :], pcol("sel", m), Y, Y.ap[:, :], ALU.mult, ALU.add)
        yield

    def rnn_chain(r, m, xb):
        tmp = rc[r]
        bank = ps[6 + r]
        xa_t, xa_ap = tmp["xcb"], tmp["xcb"].ap[:, 0:CW]
        xb_t = xcb2[r]
        xb_ap = xcb2[r].ap[:, 0:CW]
        yield from rnn_front(r, xb, bank, xa_t, xa_ap)
        g1 = rnn_back(r, m, bank, tmp, xa_t, xa_ap)
        g2 = rnn_front(r + 2, xb, bank, xb_t, xb_ap)
        next(g1)
        yield
        live = [g1, g2]
        while live:
            nxt = []
            for g in live:
                try:
                    next(g)
                    nxt.append(g)
                except StopIteration:
                    pass
            live = nxt
            yield
        yield from rnn_back(r + 2, m, bank, tmp, xb_t, xb_ap)

    def a_src(c):
        return lambda k: xT[k * 128:(k + 1) * 128, c * CW:(c + 1) * CW]

    def seq(*gens):
        for g in gens:
            yield from g

    def flagged(gen, flag):
        yield from gen
        flag[0] = True

    k_done = [0]
    q_done = [0]
    bt_ready = [False]

    def q_tail(h, parb):
        k_done[0] += 1
        while not (b_ready[0] and bt_ready[0]) or k_done[0] < 4:
            yield
        if h >= 2:
            while q_done[0] < 2:
                yield
        yield from qk_chain(h, h, xbp[parb], W, dvc.ap[:, 8:9], Ct[parb], St[parb],
                            (kc[h]["ksq"], kc[h]["knb"]), None, banks=(ps[h % 2], ps[2 + h % 2]))
        q_done[0] += 1

    def phase_a_chains(c):
        par = c % 2
        xb = xbp[par]
        ch = [qk_chain(h, 4 + h, xb, CW, pcol("kw"), Ct[par], St[par], KTt[h][c], KT[:, h, c * CW:(c + 1) * CW])
              for h in range(4)]
        if c % 4 == 3:
            k_done[0] = 0
            q_done[0] = 0
            ch = [seq(ch[h], q_tail(h, (c + 1) % 2)) for h in range(4)]
        ch.append(v_chain(c, xb))
        if c % 4 == 3:
            xbo = xbp[(c + 1) % 2]
            ch.append(seq(rnn_chain(0, c % 4, xb), gr_chain(0, xbo, None)))
            ch.append(seq(rnn_chain(1, c % 4, xb), gr_chain(1, xbo, None)))
        else:
            ch.append(rnn_chain(0, c % 4, xb))
            ch.append(rnn_chain(1, c % 4, xb))
        ch.append(bg_chain(6))
        return ch

    xmid_t = [T(None, f"xmid{n}") for n in range(NSLOT)]

    b_ready = [False]

    def gr_chain(r, xb, mixl):
        while not b_ready[0]:
            yield
        tmp = rc[r]
        gl = tmp["r"]
        bank = ps[6 + r]
        for ch in (r, r + 2):
            w = acquire()
            while w is None:
                yield
                w = acquire()
            load_w(w, 16 + ch, win_t, win_s)
            yield
            for k in range(8):
                mm(bank, bank.ap[:, 0:W], w, w.ap[:, k, :], xb[k], xb[k].ap[:, 0:W], k == 0, k == 7)
            release(w)
            yield
            act(gl, gl.ap[:, :], bank, bank.ap[:, 0:W], AF.Gelu_apprx_tanh)
            yield
            tt("dve", yown[ch], yown[ch].ap[:, :], yown[ch], yown[ch].ap[:, :], gl, gl.ap[:, :], ALU.mult)
            yield

    def phase_b(n, par):
        xb = xbp[par]
        mixl = xbp[1 - par]
        C, Sn = Ct[par], St[par]
        QT = [kc[h]["ksq"] for h in range(4)]
        QTa = [kc[h]["ksq"] for h in range(4)]
        QTb = [kc[h]["knb"] for h in range(4)]
        ysq = xsq[0]
        for ch in range(4):
            act(ysq, ysq.ap[:, :], yown[ch], yown[ch].ap[:, :], AF.Square)
            mm(ps[4], ps[4].ap[:, 0:W], ones_bf, ones_bf.ap[:, :], ysq, ysq.ap[:, :], ch == 0, ch == 3)
        ry = rc[0]["a"]
        rstd_from_psum(ps[4], W, 1.0 / 512.0, ry)
        for ch in range(4):
            tt("dve", mixl[4 + ch], mixl[4 + ch].ap[:, :], yown[ch], yown[ch].ap[:, :], ry, ry.ap[:, :], ALU.mult)
        wo = []
        for dc in range(NWST - 1):
            w = acquire()
            assert w is not None
            load_w(w, dc, wout_t, wout_s)
            wo.append(w)
        nkt = 8 * n + 8
        steps = [(h, kt) for h in range(4) for kt in range(nkt)]
        pbuf = [(rc[0]["xcb"], rc[1]["xcb"]), (xsq[0], xsq[1])]

        def qk_mm(idx):
            h, kt = steps[idx]
            c = kt // 2
            ksl = slice(kt * 128, (kt + 1) * 128)
            sA, sB = ps[(idx % 2) * 2], ps[(idx % 2) * 2 + 1]
            mm(sA, sA.ap[:, 0:W], KTt[h][c], KT[:, h, ksl], QTa[h], QTa[h].ap[:, :], True, True)
            mm(sB, sB.ap[:, 0:W], KTt[h][c], KT[:, h, ksl], QTb[h], QTb[h].ap[:, :], True, True)

        qk_mm(0)
        deferred = []
        for idx, (h, kt) in enumerate(steps):
            if idx + 1 < len(steps):
                qk_mm(idx + 1)
            sA, sB = ps[(idx % 2) * 2], ps[(idx % 2) * 2 + 1]
            p0, p1 = pbuf[idx % 2]
            act(p0, p0.ap[:, :], sA, sA.ap[:, 0:W], AF.Exp)
            act(p1, p1.ap[:, :], sB, sB.ap[:, 0:W], AF.Exp)
            mi = kt - (8 * n - 1)
            if mi >= 0:
                tt("dve", p0, p0.ap[:, :], p0, p0.ap[:, :], maskb, maskb.ap[:, mi, :], ALU.mult)
                tt("dve", p1, p1.ap[:, :], p1, p1.ap[:, :], maskb, maskb.ap[:, mi, :], ALU.mult)
            O0, O1, D0, D1 = ps[4], ps[5], ps[6], ps[7]
            vt = VVt[kt]
            vsl = VV[:, kt, h * 128:(h + 1) * 128]
            st, sp_ = kt == 0, kt == nkt - 1
            mm(O0, O0.ap[:, 0:W], vt, vsl, p0, p0.ap[:, :], st, sp_)
            mm(O1, O1.ap[:, 0:W], vt, vsl, p1, p1.ap[:, :], st, sp_)
            mm(D0, D0.ap[:, 0:W], ones_bf, ones_bf.ap[:, :], p0, p0.ap[:, :], st, sp_)
            mm(D1, D1.ap[:, 0:W], ones_bf, ones_bf.ap[:, :], p1, p1.ap[:, :], st, sp_)
            if kt == nkt - 1:
                hp = h % 2
                o0s, o1s, at, rs = rc[hp]["xc"], rc[hp]["r"], rc[hp]["ig"], rc[hp]["a"]
                rd0, rd1 = kc[hp]["rk"], kc[hp]["kn"]
                cp("dve", o0s, o0s.ap[:, :], O0, O0.ap[:, 0:W])
                cp("dve", o1s, o1s.ap[:, :], O1, O1.ap[:, 0:W])
                act(rd0, rd0.ap[:, :], D0, D0.ap[:, 0:W], AF.Ln, bias=epsb.ap[:, 2:3], extra_reads=[epsb])
                act(rd1, rd1.ap[:, :], D1, D1.ap[:, 0:W], AF.Ln, bias=epsb.ap[:, 2:3], extra_reads=[epsb])

                def rest(h=h, o0s=o0s, o1s=o1s, rd0=rd0, rd1=rd1):
                    act(rd0, rd0.ap[:, :], rd0, rd0.ap[:, :], AF.Exp, scale=-1.0)
                    act(rd1, rd1.ap[:, :], rd1, rd1.ap[:, :], AF.Exp, scale=-1.0)
                    tt("dve", o0s, o0s.ap[:, :], o0s, o0s.ap[:, :], rd0, rd0.ap[:, :], ALU.mult)
                    tt("dve", o1s, o1s.ap[:, :], o1s, o1s.ap[:, :], rd1, rd1.ap[:, :], ALU.mult)
                    stt(mixl[h], mixl[h].ap[:, :], o1s, o1s.ap[:, :], dvc.ap[:, 9:10], o0s, o0s.ap[:, :], ALU.mult, ALU.add)

                deferred.append((idx + 3, rest))
            while deferred and deferred[0][0] <= idx:
                deferred.pop(0)[1]()
        while deferred:
            deferred.pop(0)[1]()
        asqs = [pbuf[h % 2][h // 2] for h in range(4)]
        bks = [ps[4 + h] for h in range(4)]
        rss = [rc[h % 2]["a"] if h < 2 else rc[h % 2]["ig"] for h in range(4)]
        for h in range(4):
            act(asqs[h], asqs[h].ap[:, :], mixl[h], mixl[h].ap[:, :], AF.Square)
        for h in range(4):
            mm(bks[h], bks[h].ap[:, 0:W], ones_bf, ones_bf.ap[:, :], asqs[h], asqs[h].ap[:, :], True, True)
        for h in range(4):
            act(rss[h], rss[h].ap[:, :], bks[h], bks[h].ap[:, 0:W], AF.Ln, bias=epsb.ap[:, 0:1], scale=1.0 / 128.0,
                extra_reads=[epsb])
        for h in range(4):
            act(rss[h], rss[h].ap[:, :], rss[h], rss[h].ap[:, :], AF.Exp, scale=-0.5)
        for h in range(4):
            tt("dve", mixl[h], mixl[h].ap[:, :], mixl[h], mixl[h].ap[:, :], rss[h], rss[h].ap[:, :], ALU.mult)
        for dc in range(8):
            w = wo[dc]
            po = ps[dc % 2]
            for k in range(8):
                mm(po, po.ap[:, 0:W], w, w.ap[:, k, :], mixl[k], mixl[k].ap[:, :], k == 0, k == 7)
            release(w)
            if len(wo) < 8:
                w2 = acquire()
                assert w2 is not None
                load_w(w2, len(wo), wout_t, wout_s)
                wo.append(w2)
            xm = rTa if dc % 2 else rTb
            tt("dve", xm, xm.ap[:, :], po, po.ap[:, 0:W], xs[dc], xs[dc].ap[:, 0:W], ALU.add)
            dma(xmid_t[n], xmid_s[n][:, dc * W:(dc + 1) * W], xm.ap[:, :], reads=[xm], nowaw=True)

    run_chains([stage0(a_src(0), CW, xbp[0], ps[5]), gen_tables(0, CW, Ct[0], St[0])])
    for c in range(NCH):
        chains = phase_a_chains(c)
        if c % 4 != 3 and c + 1 < NCH:
            chains.append(stage0(a_src(c + 1), CW, xbp[(c + 1) % 2], ps[5]))
            chains.append(gen_tables(c + 1, CW, Ct[(c + 1) % 2], St[(c + 1) % 2]))
        if c % 4 == 3:
            nb = c // 4
            b_ready[0] = False
            chains.append(flagged(stage0(lambda k, nb=nb: xoT[k * 128:(k + 1) * 128, nb * W:(nb + 1) * W], W,
                                         xbp[(c + 1) % 2], ps[5]), b_ready))
            bt_ready[0] = False
            chains.append(flagged(gen_tables(32 + nb, W, Ct[(c + 1) % 2], St[(c + 1) % 2]), bt_ready))
        run_chains(chains)
        if c % 4 == 3:
            phase_b(c // 4, (c + 1) % 2)
            if c + 1 < NCH:
                run_chains([stage0(a_src(c + 1), CW, xbp[(c + 1) % 2], ps[5]),
                            gen_tables(c + 1, CW, Ct[(c + 1) % 2], St[(c + 1) % 2])])
    for _ in bg:
        pass

    last = {e: (P.ins[e][-1] if P.ins[e] else None) for e in P.ENGS}

    def otile(ap, name):
        t = T(ap, name)
        t.readers = {e: i for e, i in last.items() if i is not None and e != "sp"}
        return t

    WDN = otile(arena[:, 0:24576].rearrange("p (k x) -> p k x", x=1024), "WDN")
    MT = [[otile(arena[:, 24576 + (g * 24 + ci) * 256:24576 + (g * 24 + ci + 1) * 256], f"mt{g}_{ci}")
           for ci in range(24)] for g in range(4)]
    H2 = [[otile(arena[:, 49152 + (g * 8 + k) * W:49152 + (g * 8 + k + 1) * W], f"h2{g}_{k}")
           for k in range(8)] for g in range(4)]
    out_t = T(None, "outT")
    for dc in range(8):
        dma(WDN, WDN.ap[:, :, dc * 128:(dc + 1) * 128], wdn_s[dc].rearrange("p (k x) -> p k x", x=128),
            reads=[wdn_t], nowaw=True)
    cfo, _ = PC["cfw"]
    cbo, _ = PC["cfb"]

    def ffn_chain(i, grp, cis):
        ur = [kc[2 * i]["rk"], kc[2 * i]["kn"]]
        cv = [kc[2 * i + 1]["rk"], kc[2 * i + 1]["kn"]]
        gl = rc[i]["xc"]
        cis = list(cis)
        nxt = None
        for ii, ci in enumerate(cis):
            if nxt is not None and len(nxt) == 2:
                wpair = nxt
            else:
                wpair = nxt or []
                for cc in (ci, 24 + ci)[len(wpair):]:
                    w = acquire()
                    while w is None:
                        yield
                        w = acquire()
                    load_w(w, cc, wup_t, wup_s)
                    wpair.append((cc, w))
            nxt = []
            if ii + 1 < len(cis):
                for cc in (cis[ii + 1], 24 + cis[ii + 1]):
                    w = acquire()
                    if w is None:
                        break
                    load_w(w, cc, wup_t, wup_s)
                    nxt.append((cc, w))
            yield
            for g in range(4):
                n = grp * 4 + g
                for hi, (cc, w) in enumerate(wpair):
                    pu = ps[2 * i + hi]
                    for k in range(8):
                        mm(pu, pu.ap[:, 0:W], w, w.ap[:, k, :], H2[g][k], H2[g][k].ap, k == 0, k == 7)
                    yield
                    wc = lambda tap, cc=cc: prm.ap[:, cfo + cc * 3 + tap:cfo + cc * 3 + tap + 1]
                    if n == 0:
                        act(ur[hi], ur[hi].ap[:, 0:257], pu, pu.ap[:, 0:257], AF.Copy)
                        act(cv[hi], cv[hi].ap[:, 0:CW], pu, pu.ap[:, 2:258], AF.Identity, scale=wc(2),
                            bias=prm.ap[:, cbo + cc:cbo + cc + 1])
                        ts("dve", ur[hi], ur[hi].ap[:, 0:2], ur[hi], ur[hi].ap[:, 0:2], pcol("hflag"), None, ALU.mult)
                        yield
                        stt(cv[hi], cv[hi].ap[:, 0:CW], ur[hi], ur[hi].ap[:, 1:257], wc(1), cv[hi], cv[hi].ap[:, 0:CW], ALU.mult, ALU.add)
                        yield
                        stt(cv[hi], cv[hi].ap[:, 0:CW], ur[hi], ur[hi].ap[:, 0:256], wc(0), cv[hi], cv[hi].ap[:, 0:CW], ALU.mult, ALU.add)
                        yield
                    else:
                        act(cv[hi], cv[hi].ap[:, 0:CW], pu, pu.ap[:, 2:258], AF.Identity, scale=wc(2),
                            bias=prm.ap[:, cbo + cc:cbo + cc + 1])
                        yield
                        stt(cv[hi], cv[hi].ap[:, 0:CW], pu, pu.ap[:, 1:257], wc(1), cv[hi], cv[hi].ap[:, 0:CW], ALU.mult, ALU.add)
                        yield
                        stt(cv[hi], cv[hi].ap[:, 0:CW], pu, pu.ap[:, 0:256], wc(0), cv[hi], cv[hi].ap[:, 0:CW], ALU.mult, ALU.add)
                        yield
                act(gl, gl.ap[:, 0:CW], cv[0], cv[0].ap[:, 0:CW], AF.Gelu_apprx_tanh)
                yield
                tt("pool", MT[g][ci], MT[g][ci].ap, gl, gl.ap[:, 0:CW], cv[1], cv[1].ap[:, 0:CW], ALU.mult)
                yield
            for _, w in wpair:
                release(w)

    def down_chain(i, grp):
        xm, ob = rc[i]["r"], rc[i]["ig"]
        pd = ps[6 + i]
        for g in range(4):
            n = grp * 4 + g
            for dc in range(i, 8, 2):
                dma(xm, xm.ap[:, :], xmid_s[n][:, dc * W:(dc + 1) * W], reads=[xmid_t[n]])
                for ci in range(24):
                    mm(pd, pd.ap[:, 0:CW], WDN, WDN.ap[:, ci, dc * 128:(dc + 1) * 128], MT[g][ci], MT[g][ci].ap, ci == 0, ci == 23)
                    if ci % 8 == 7:
                        yield
                tt("dve", ob, ob.ap[:, 0:CW], pd, pd.ap[:, 0:CW], xm, xm.ap[:, 2:258], ALU.add)
                yield
                dma(out_t, outT[dc * 128:(dc + 1) * 128, n * CW:(n + 1) * CW], ob.ap[:, 0:CW], reads=[ob], nowaw=True)
                yield

    for grp in range(2):
        for g in range(4):
            n = grp * 4 + g
            for k in range(8):
                dma(xs[k], xs[k].ap[:, 0:W], xmid_s[n][:, k * W:(k + 1) * W], reads=[xmid_t[n]])
            for k in range(8):
                q = xsq[k % 2]
                act(q, q.ap[:, 0:W], xs[k], xs[k].ap[:, 0:W], AF.Square)
                mm(ps[5], ps[5].ap[:, 0:W], ones_bf, ones_bf.ap[:, :], q, q.ap[:, 0:W], k == 0, k == 7)
            rstd_from_psum(ps[5], W, 1.0 / 1024.0, rstd_t)
            for k in range(8):
                tt("pool" if k % 4 == 3 else "dve", H2[g][k], H2[g][k].ap, xs[k], xs[k].ap[:, 0:W],
                   rstd_t, rstd_t.ap[:, 0:W], ALU.mult)
        run_chains([ffn_chain(0, grp, range(0, 24, 2)), ffn_chain(1, grp, range(1, 24, 2))])
        run_chains([down_chain(0, grp), down_chain(1, grp)])
    P.emit([out_t])
    return nc, es


_CACHE = {}


def _params_for(inp, j):
    p = np.zeros((128, NP), np.float32)

    def put(name, arr):
        o, w = PC[name]
        arr = np.asarray(arr, np.float32)
        assert arr.shape == (128, w), (name, arr.shape)
        p[:, o:o + w] = arr

    put("n1w", inp["norm1_w"][0].reshape(8, 128).T)
    put("qw", np.tile(inp["q_norm_w"][0], 2)[:, None])
    put("kw", np.tile(inp["k_norm_w"][0], 2)[:, None])
    lam = np.zeros((128, 4), np.float32)
    lam[:64, 0] = inp["lambda_q1"][0]
    lam[:64, 1] = inp["lambda_k1"][0]
    lam[:64, 2] = inp["lambda_q2"][0]
    lam[:64, 3] = inp["lambda_k2"][0]
    put("lam", lam)
    put("subw", inp["subln_w"][0][:, None])
    crw = inp["conv_rnn_w"][0]
    put("crw", crw.reshape(4, 4, 128).transpose(2, 1, 0).reshape(128, 16))
    put("crb", inp["conv_rnn_b"][0].reshape(4, 128).T)
    put("bga", inp["b_gate_a"][0].reshape(4, 128).T)
    put("bgx", inp["b_gate_x"][0].reshape(4, 128).T)
    put("lru", inp["lru_lambda"][0].reshape(4, 128).T)
    put("rnw", inp["rnn_norm_w"][0].reshape(4, 128).T)
    put("n2w", inp["norm2_w"][0].reshape(8, 128).T)
    cfw = inp["conv_ffn_w"][0]
    put("cfw", cfw.reshape(3, 48, 128).transpose(2, 1, 0).reshape(128, 144))
    put("cfb", inp["conv_ffn_b"][0].reshape(48, 128).T)
    sel = np.zeros((128, 4), np.float32)
    sel[:, j] = 1.0
    put("sel", sel)
    put("hflag", np.full((128, 1), 0.0 if j == 0 else 1.0, np.float32))
    put("qpos", np.tile((256.0 * (4 * np.arange(8) + j) - 2.0)[None, :], (128, 1)))
    put("kpos", np.tile((256.0 * np.arange(32))[None, :], (128, 1)))
    ft = np.zeros((128, 1), np.float32)
    for pp in range(128):
        i = pp % 64
        if i < 16:
            ft[pp, 0] = (500000.0 ** (-(2.0 * (i % 8)) / 16.0)) / (2.0 * np.pi)
    put("fturn", ft)
    return p


def _consts():
    rot = np.zeros((128, 128), np.float32)
    for blk in (0, 64):
        for i in range(8):
            rot[blk + i + 8, blk + i] = -1.0
            rot[blk + i, blk + i + 8] = 1.0
    return np.concatenate([rot, np.eye(128, dtype=np.float32)], axis=1)


def _mask(j):
    m = np.zeros((128, 9, W), np.float32)
    r = np.arange(128)[:, None]
    col = np.arange(W)[None, :]
    qrel = 256 * j - 2 + col
    for mi in range(9):
        krel = 128 * (mi - 1) + r
        m[:, mi, :] = (krel <= qrel).astype(np.float32)
    return m.reshape(128, 9 * W)


def kernel(**inp):
    x = np.asarray(inp["x"], np.float32)
    if "nc" not in _CACHE:
        _CACHE["nc"] = build()
    nc, _es = _CACHE["nc"]
    wg = np.zeros((128, 8, 128), np.float32)
    for gi, key in enumerate(("w_gate_a", "w_gate_x")):
        wgt = np.asarray(inp[key][0], np.float32)
        for ch in range(4):
            for hb in range(2):
                wg[hb * 64:(hb + 1) * 64, gi * 4 + ch, hb * 64:(hb + 1) * 64] = wgt[ch * 2 + hb]
    wg = wg.reshape(128, 1024)
    cst = _consts()
    in_maps = []
    for core in range(8):
        b, j = divmod(core, 4)
        xTb = np.ascontiguousarray(x[b].T)
        xo = np.zeros((D, NSLOT, W), np.float32)
        for n in range(NSLOT):
            s0 = 256 * (4 * n + j)
            if s0 >= 2:
                xo[:, n, :] = xTb[:, s0 - 2:s0 + 256]
            else:
                xo[:, n, 2:] = xTb[:, 0:256]
        in_maps.append({
            "xT": xTb, "xoT": xo.reshape(D, NSLOT * W),
            "w_in": np.ascontiguousarray(inp["w_in"][0], np.float32),
            "w_out": np.ascontiguousarray(inp["w_out"][0], np.float32),
            "w_up": np.ascontiguousarray(inp["w_up"][0], np.float32),
            "w_down": np.ascontiguousarray(inp["w_down"][0], np.float32),
            "params": _params_for(inp, j), "wg": wg, "cst": cst, "mask": _mask(j),
        })
    res = run_bass_kernel_spmd(nc, in_maps, core_ids=list(range(8)))
    out = np.zeros((2, S, D), np.float32)
    for core in range(8):
        b, j = divmod(core, 4)
        oT = np.asarray(res.results[core]["outT"]).reshape(D, NSLOT, CW)
        for n in range(NSLOT):
            s0 = 256 * (4 * n + j)
            out[b, s0:s0 + 256, :] = oT[:, n, :].T
    return out
```
